# Optimizing a Trainium2 kernel written in Bass

```python
import jax, jax.numpy as jnp
from jax import lax
import numpy as np

D_MODEL = 1024
BATCH = 2
SEQ = 8192
DEPTH = 4

N_MIXERS = 3
CONF_CONV_WIDTH = 31
POOL_WINDOWS = (2, 4, 8, 16)
POOL_GROUPS = len(POOL_WINDOWS)
POOL_GROUP_DIM = D_MODEL // POOL_GROUPS
SHORT_CONV_WIDTH = 3
FFN_CONV_WIDTH = 3
D_FF = 2816
RMS_EPS = 1e-6
LN_EPS = 1e-5
N_A = (DEPTH + N_MIXERS - 1) // N_MIXERS
N_B = (DEPTH + N_MIXERS - 2) // N_MIXERS
N_C = (DEPTH + N_MIXERS - 3) // N_MIXERS

kernel_name = "hybrid_interleaved_conv_pool_shortconv_trunk"


def rmsnorm(x, g):
    x32 = x.astype(jnp.float32)
    y = x32 * lax.rsqrt(jnp.mean(x32 * x32, axis=-1, keepdims=True) + RMS_EPS)
    return y.astype(x.dtype) * g


def layernorm(x, g, b):
    x32 = x.astype(jnp.float32)
    mu = jnp.mean(x32, axis=-1, keepdims=True)
    var = jnp.mean(jnp.square(x32 - mu), axis=-1, keepdims=True)
    y = (x32 - mu) * lax.rsqrt(var + LN_EPS)
    return y.astype(x.dtype) * g + b


def causal_dwconv(x, w):
    k, ch = w.shape
    return lax.conv_general_dilated(
        x, w[:, None, :].astype(x.dtype), window_strides=(1,), padding=[(k - 1, 0)],
        dimension_numbers=("NWC", "WIO", "NWC"), feature_group_count=ch)


def conformer_conv_module(h, w1, b1, dw_w, dw_b, ln_g, ln_b, w2, b2):
    u = h @ w1 + b1
    a, gt = jnp.split(u, 2, axis=-1)
    u = a * jax.nn.sigmoid(gt)
    u = causal_dwconv(u, dw_w) + dw_b
    u = jax.nn.silu(layernorm(u, ln_g, ln_b))
    return u @ w2 + b2


def pooling_mixer(h, group_w, group_b, ch_scale):
    bsz, t_len, d = h.shape
    h32 = h.astype(jnp.float32)
    cs = jnp.cumsum(h32, axis=1)
    pos = jnp.arange(1, t_len + 1, dtype=jnp.float32)[None, :, None]
    means = []
    for g, w in enumerate(POOL_WINDOWS):
        csg = cs[..., g * POOL_GROUP_DIM:(g + 1) * POOL_GROUP_DIM]
        prev = jnp.pad(csg, ((0, 0), (w, 0), (0, 0)))[:, :t_len]
        means.append((csg - prev) / jnp.minimum(pos, float(w)))
    pooled = (jnp.concatenate(means, axis=-1) - h32).astype(h.dtype)
    pooled = pooled.reshape(bsz, t_len, POOL_GROUPS, POOL_GROUP_DIM)
    mixed = jnp.einsum("btgc,gcd->btgd", pooled, group_w).reshape(bsz, t_len, d)
    return (mixed + group_b) * ch_scale


def short_conv_mixer(h, w_in, w_conv, w_out):
    bcx = h @ w_in
    gb, gc, v = jnp.split(bcx, 3, axis=-1)
    return (gb * causal_dwconv(gc * v, w_conv)) @ w_out


def conv_ffn(h, w_up, dw_w, dw_b, w_down):
    u = causal_dwconv(h @ w_up, dw_w) + dw_b
    gt, v = jnp.split(u, 2, axis=-1)
    return (jax.nn.silu(gt) * v) @ w_down


def setup_inputs(seed: int = 0) -> dict:
    key = jax.random.key(seed)
    ks = iter(jax.random.split(key, 40))
    d, f = D_MODEL, D_FF

    def nrm(shape, scale):
        return scale * jax.random.normal(next(ks), shape, jnp.float32)

    def gain(shape):
        return 1.0 + nrm(shape, 0.05)

    return {
        "x": nrm((BATCH, SEQ, d), 1.0),
        "c": nrm((BATCH, d), 1.0),
        "mod_w": nrm((DEPTH, d, 6 * d), 0.5 * d ** -0.5),
        "mod_b": nrm((DEPTH, 6 * d), 0.02),
        "norm_pre_mix": gain((DEPTH, d)),
        "norm_post_mix": gain((DEPTH, d)),
        "norm_pre_ffn": gain((DEPTH, d)),
        "norm_post_ffn": gain((DEPTH, d)),
        "a_pw1_w": nrm((N_A, d, 2 * d), d ** -0.5),
        "a_pw1_b": nrm((N_A, 2 * d), 0.02),
        "a_dw_w": nrm((N_A, CONF_CONV_WIDTH, d), CONF_CONV_WIDTH ** -0.5),
        "a_dw_b": nrm((N_A, d), 0.02),
        "a_ln_g": gain((N_A, d)),
        "a_ln_b": nrm((N_A, d), 0.02),
        "a_pw2_w": nrm((N_A, d, d), d ** -0.5),
        "a_pw2_b": nrm((N_A, d), 0.02),
        "b_group_w": nrm((N_B, POOL_GROUPS, POOL_GROUP_DIM, POOL_GROUP_DIM), POOL_GROUP_DIM ** -0.5),
        "b_group_b": nrm((N_B, d), 0.02),
        "b_scale": gain((N_B, d)),
        "c_in_w": nrm((N_C, d, 3 * d), d ** -0.5),
        "c_conv_w": nrm((N_C, SHORT_CONV_WIDTH, d), SHORT_CONV_WIDTH ** -0.5),
        "c_out_w": nrm((N_C, d, d), d ** -0.5),
        "f_up_w": nrm((DEPTH, d, 2 * f), d ** -0.5),
        "f_dw_w": nrm((DEPTH, FFN_CONV_WIDTH, 2 * f), FFN_CONV_WIDTH ** -0.5),
        "f_dw_b": nrm((DEPTH, 2 * f), 0.02),
        "f_down_w": nrm((DEPTH, f, d), f ** -0.5),
    }


def reference(x, c, mod_w, mod_b, norm_pre_mix, norm_post_mix, norm_pre_ffn, norm_post_ffn,
              a_pw1_w, a_pw1_b, a_dw_w, a_dw_b, a_ln_g, a_ln_b, a_pw2_w, a_pw2_b,
              b_group_w, b_group_b, b_scale,
              c_in_w, c_conv_w, c_out_w,
              f_up_w, f_dw_w, f_dw_b, f_down_w):
    c_act = jax.nn.silu(c)
    for i in range(DEPTH):
        mod = c_act @ mod_w[i] + mod_b[i]
        sh_m, sc_m, gt_m, sh_f, sc_f, gt_f = [m[:, None, :] for m in jnp.split(mod, 6, axis=-1)]

        h = rmsnorm(x, norm_pre_mix[i]) * (1.0 + sc_m) + sh_m
        kind, slot = i % N_MIXERS, i // N_MIXERS
        if kind == 0:
            y = conformer_conv_module(h, a_pw1_w[slot], a_pw1_b[slot], a_dw_w[slot], a_dw_b[slot],
                                      a_ln_g[slot], a_ln_b[slot], a_pw2_w[slot], a_pw2_b[slot])
        elif kind == 1:
            y = pooling_mixer(h, b_group_w[slot], b_group_b[slot], b_scale[slot])
        else:
            y = short_conv_mixer(h, c_in_w[slot], c_conv_w[slot], c_out_w[slot])
        x = x + gt_m * rmsnorm(y, norm_post_mix[i])

        h = rmsnorm(x, norm_pre_ffn[i]) * (1.0 + sc_f) + sh_f
        y = conv_ffn(h, f_up_w[i], f_dw_w[i], f_dw_b[i], f_down_w[i])
        x = x + gt_f * rmsnorm(y, norm_post_ffn[i])
    return x
```

```python
import numpy as np
import concourse.bass as bass
import concourse.mybir as mybir
from concourse.bass_utils import run_bass_kernel_spmd

F32 = mybir.dt.float32
BF16 = mybir.dt.bfloat16
F32R = mybir.dt.float32r
AF = mybir.ActivationFunctionType
ALU = mybir.AluOpType

D = 1024
KC = 8
FF = 2816
NPAIR = 22
DEPTH = 4
SEQ = 8192
BATCH = 2
NCORES = 8
NVC = 2
SV = 1024
H = 86
S = SV + H
NT = 370
NSUB = 3
LM = 32
RMS_EPS = 1e-6
LN_EPS = 1e-5
MODP = 256
NMODP = 6 * D // MODP
NHP = NPAIR // 2

SAME_ENGINE_SYNC = True
MASK_ENG = "dve"
PN_ENG = "dve"
GM_ENG = "pool"
NWS = 4
WDEPTH = 3


def _vec_layout():
    off = {}
    n = 0

    def put(name, w):
        nonlocal n
        off[name] = n
        n += w
    for i in range(DEPTH):
        for nm in ("npm", "nqm", "npf", "nqf"):
            put(f"{nm}{i}", 8)
        put(f"modb{i}", 48)
        put(f"fdw{i}", 3 * 44)
        put(f"fdb{i}", 44)
    for s in range(2):
        put(f"ab1{s}", 16)
        put(f"adw{s}", 31 * 8)
        put(f"adb{s}", 8)
        put(f"alg{s}", 8)
        put(f"alb{s}", 8)
        put(f"ab2{s}", 8)
    put("bgb", 8)
    put("bsc", 8)
    put("ccw", 24)
    return off, n


VOFF, NV = _vec_layout()


def _pk(v):
    v = np.asarray(v, np.float32)
    return np.ascontiguousarray(v.reshape(-1, 128).T)


class Op:
    __slots__ = ("eng", "emit", "reads", "writes", "dma", "waits", "signal", "count", "idx")

    def __init__(self, eng, emit, reads, writes, dma):
        self.eng = eng
        self.emit = emit
        self.reads = reads
        self.writes = writes
        self.dma = dma
        self.waits = []
        self.signal = False
        self.count = 0


class Sched:
    ENGS = ("pe", "act", "dve", "pool", "sp")

    def __init__(self):
        self.ops = []

    def add(self, eng, emit, reads=(), writes=(), dma=None):
        op = Op(eng, emit, tuple(reads), tuple(writes), dma)
        op.idx = len(self.ops)
        self.ops.append(op)
        return op

    def stream_of(self, op):
        return ("dma", op.dma) if op.dma is not None else op.eng

    def analyze(self):
        last_w = {}
        readers = {}
        waited = {e: {} for e in self.ENGS}
        for i, op in enumerate(self.ops):
            deps = set()
            for k in op.reads:
                if k in last_w:
                    deps.add(last_w[k])
            for k in op.writes:
                if k in last_w:
                    deps.add(last_w[k])
                r = readers.get(k)
                if r:
                    deps.update(r)
            deps.discard(i)
            best = {}
            for d in deps:
                dop = self.ops[d]
                st = self.stream_of(dop)
                if dop.dma is None and dop.eng == op.eng and op.dma is None:
                    if op.eng == "pe" or not SAME_ENGINE_SYNC:
                        continue
                if st not in best or best[st] < d:
                    best[st] = d
            w = waited[op.eng]
            for st, d in best.items():
                if w.get(st, -1) >= d:
                    continue
                w[st] = d
                op.waits.append(d)
                self.ops[d].signal = True
            for k in op.reads:
                readers.setdefault(k, []).append(i)
            for k in op.writes:
                last_w[k] = i
                readers[k] = []
        cnt = {}
        for op in self.ops:
            st = self.stream_of(op)
            if op.dma is not None:
                op.signal = True
                cnt[st] = cnt.get(st, 0) + 16
                op.count = cnt[st]
            elif op.signal:
                cnt[st] = cnt.get(st, 0) + 1
                op.count = cnt[st]

    def emit_all(self, nc, block, sems):
        per = {e: [] for e in self.ENGS}
        for op in self.ops:
            per[op.eng].append(op)

        def run(eng_name, e):
            for op in per[eng_name]:
                for d in op.waits:
                    dop = self.ops[d]
                    e.wait_ge(sems[self.stream_of(dop)], dop.count)
                if op.emit is None:
                    continue
                ins = op.emit(e)
                if op.signal:
                    ins.then_inc(sems[self.stream_of(op)], 16 if op.dma is not None else 1)

        @block.tensor
        def _(e):
            run("pe", e)

        @block.scalar
        def _(e):
            run("act", e)

        @block.vector
        def _(e):
            run("dve", e)

        @block.gpsimd
        def _(e):
            run("pool", e)

        @block.sync
        def _(e):
            run("sp", e)


class Cfg:
    def __init__(self, nvc=NVC, nlayers=DEPTH, dumps=(), stop_after=None, mod_bg=True):
        self.nvc = nvc
        self.nlayers = nlayers
        self.dumps = tuple(dumps)
        self.stop_after = stop_after
        self.mod_bg = mod_bg


def build_program(cfg):
    nc = bass.Bass("TRN2", target_bir_lowering=False)
    sch = Sched()
    dram = {}

    def din(name, shape, dt=F32):
        dram[name] = nc.dram_tensor(name, list(shape), dt, kind="ExternalInput").ap()
        return dram[name]

    xin = din("xin", [NVC, S, D])
    wup = din("wup", [DEPTH, NPAIR, 128, 2048])
    wdn = din("wdn", [DEPTH, KC, 128, NPAIR * 128])
    aw1 = din("aw1", [2, KC, 128, 2048])
    aw2 = din("aw2", [2, KC, 128, 1024])
    cwin = din("cwin", [KC, 128, 3072])
    cwout = din("cwout", [KC, 128, 1024])
    bgw = din("bgw", [128, 2048])
    modw = din("modw", [DEPTH, NMODP, 128, KC * MODP])
    vecs_d = din("vecs", [128, NV])
    cvec_d = din("cvec", [128, KC])
    ident_d = din("ident", [128, 128])
    mask_d = din("mask", [NVC, 128, LM + H])
    invc_d = din("invc", [NVC, 128, 64])
    out_d = nc.dram_tensor("out", [NVC, SV, D], F32, kind="ExternalOutput").ap()
    dump_d = {}
    for nm in cfg.dumps:
        dump_d[nm] = nc.dram_tensor("dump_" + nm, [128, KC, S], F32, kind="ExternalOutput").ap()

    ctx = []

    def sb(name, shape, dt):
        g = nc.sbuf_tensor(name, list(shape), dt)
        t = g.__enter__()
        ctx.append(g)
        return t

    xres = sb("xres", [128, KC, S], F32)
    ybuf = sb("ybuf", [128, KC, S], F32)
    hb = sb("hb", [128, KC, LM + S], BF16)
    ARENA = 6112
    arena = sb("arena", [128, ARENA], F32)
    wsl = [sb(f"wslot{i}", [128, KC * 384], BF16) for i in range(NWS)]
    MSLOT = 6
    MDEPTH = 5
    mslot = [sb(f"mslot{i}", [128, 4 * MODP], F32R) for i in range(MSLOT)]
    vecs = sb("vecs_sb", [128, NV], F32)
    lvec = sb("lvec", [128, DEPTH, 48], F32)
    modT = sb("modT", [128, DEPTH, 48], F32)
    cact = sb("cact", [128, KC], F32)
    cact_r = sb("cact_r", [128, KC], F32R)
    ident = sb("ident_sb", [128, 128], F32)
    identb = sb("identb", [128, 128], BF16)
    onesm = sb("onesm", [128, 128], BF16)
    one11 = sb("one11", [1, 2], F32)
    epsr = sb("epsr", [128, 1], F32)
    epsl = sb("epsl", [128, 1], F32)
    maskt = sb("maskt", [128, LM + H], F32)
    invct = sb("invct", [128, 64], F32)
    bgws = sb("bgws", [128, 2048], BF16)
    modrow = sb("modrow", [1, MODP], F32)
    bsv = sb("bsv", [128, KC], F32)
    NTMP = 8
    TW = NT + LM
    tmpf = [sb(f"tmpf{i}", [128, TW], F32) for i in range(NTMP)]
    sqb = [sb(f"sqb{i}", [128, KC, NT], BF16) for i in range(2)]
    rstd_t = [sb(f"rstd{i}", [128, NT], F32) for i in range(2)]
    nmr_t = [sb(f"nmr{i}", [128, NT], F32) for i in range(2)]
    sd_t = [sb(f"sd{i}", [128, NT], F32) for i in range(2)]

    gbuf = arena[:, 0:NHP * S // 2].bitcast(BF16).rearrange("p (j s) -> p j s", j=NHP)
    GLW = NT + 30
    glu = [arena[:, i * (GLW // 2):(i + 1) * (GLW // 2)].bitcast(BF16) for i in range(2)]
    DG_OFF = GLW
    DGW = 31 * 128 // 2
    diag = [arena[:, DG_OFF + i * DGW: DG_OFF + (i + 1) * DGW].bitcast(BF16).rearrange("p (k c) -> p k c", k=31)
            for i in range(2)]
    assert DG_OFF + 2 * DGW <= ARENA, (DG_OFF + 2 * DGW, ARENA)
    rbuf = arena[:, 0:KC * S // 2].bitcast(BF16).rearrange("p (k s) -> p k s", k=KC)
    stg = [arena[:, i * D:(i + 1) * D] for i in range(2)]
    assert NHP * S // 2 <= ARENA and KC * S // 2 <= ARENA
    u2 = ybuf
    hf = ybuf
    ARENA_KEYS = ([("g", j, n) for j in range(NHP) for n in range(NSUB)] + [("glu", 0), ("glu", 1), ("diag", 0),
                  ("diag", 1)] + [("r", k, n) for k in range(KC) for n in range(NSUB)] +
                  [("stg", s_, h_) for s_ in range(2) for h_ in range(2)])

    psg = [nc.psum_tensor(f"ps{i}", [128, 512], F32) for i in range(8)]
    ps = []
    for g in psg:
        ps.append(g.__enter__())
        ctx.append(g)
    bank_ctr = [0]
    nbanks = [7]

    def nb():
        b = bank_ctr[0] % nbanks[0]
        bank_ctr[0] += 1
        return b

    tmp_ctr = [0]

    def ntmp():
        t = tmp_ctr[0] % NTMP
        tmp_ctr[0] += 1
        return t

    def V(name, w=1, i=0):
        o = VOFF[name] + i
        return vecs[:, o:o + w]

    def dma(eng, out, in_, reads, writes, key):
        sch.add(eng, lambda e: e.dma_start(out=out, in_=in_), reads, writes, dma=key)

    def dma_cast(out2d, in2d, n, reads, writes, key):
        nsp = (n + 2047) // 2048
        step = (n + nsp - 1) // nsp
        a = 0
        while a < n:
            b = min(n, a + step)
            dma("pool", out2d[:, a:b], in2d[:, a:b], reads, writes, key)
            a = b

    def mm(out, lhsT, rhs, start, stop, reads, writes):
        sch.add("pe", lambda e: e.matmul(out, lhsT=lhsT, rhs=rhs, start=start, stop=stop), reads, writes)

    def act(out, in_, func, reads, writes, scale=1.0, bias=0.0):
        sch.add("act", lambda e: e.activation(out=out, in_=in_, func=func, bias=bias, scale=scale), reads, writes)

    def tt(eng, out, in0, in1, op, reads, writes):
        sch.add(eng, lambda e: e.tensor_tensor(out=out, in0=in0, in1=in1, op=op), reads, writes)

    def stt(out, in0, scalar, in1, op0, op1, reads, writes, eng="dve"):
        sch.add(eng, lambda e: e.scalar_tensor_tensor(out=out, in0=in0, scalar=scalar, in1=in1, op0=op0, op1=op1),
                reads, writes)

    def ts(eng, out, in0, s1, s2, op0, op1, reads, writes):
        sch.add(eng, lambda e: e.tensor_scalar(out=out, in0=in0, scalar1=s1, scalar2=s2, op0=op0, op1=op1),
                reads, writes)

    def cp(eng, out, in_, reads, writes):
        if eng == "act":
            sch.add("act", lambda e: e.copy(out=out, in_=in_), reads, writes)
        else:
            sch.add(eng, lambda e: e.tensor_copy(out=out, in_=in_), reads, writes)

    def memset(eng, ap, val, writes):
        sch.add(eng, lambda e: e.memset(ap, val), (), writes)

    def barrier(reads, writes, eng="pool"):
        sch.add(eng, lambda e: e.nop(), reads, writes)

    dma("sp", vecs[:], vecs_d, (), [("vecs",)], "c0")
    dma("sp", cact[:], cvec_d, (), [("cact",)], "c1")
    dma("sp", ident[:], ident_d, (), [("ident",)], "c2")
    dma_cast(bgws[:], bgw, 2048, (), [("bgws",)], "c3")
    memset("dve", onesm[:], 1.0 / 1024.0, [("onesm",)])
    memset("dve", one11[:], 1.0, [("one11",)])
    memset("dve", epsr[:], RMS_EPS, [("epsr",)])
    memset("dve", epsl[:], LN_EPS, [("epsl",)])
    memset("dve", hb[:, :, 0:LM], 0.0, [("hm",)])
    cp("dve", identb[:], ident[:], [("ident",)], [("identb",)])
    tt("dve", bsv[:, :], V("bgb", 8), V("bsc", 8), ALU.mult, [("vecs",)], [("bsv",)])
    act(cact_r[:], cact[:], AF.Silu, [("cact",)], [("cact_r",)])

    def mod_piece(i, cb, kh):
        sl = len(mp) % MSLOT
        kslot = ("mslot", sl)
        mp.append((len(sch.ops), sl, modw[i, cb][:, kh * 4 * MODP:(kh + 1) * 4 * MODP]))
        b = 7
        for q in range(4):
            kc = kh * 4 + q
            mm(ps[b][0:1, 0:MODP], cact_r[:, kc:kc + 1],
               mslot[sl][:, q * MODP:(q + 1) * MODP], kc == 0, kc == KC - 1,
               [kslot, ("cact_r",)], [("ps", b)])
        if kh == 1:
            cp("act", modrow[0:1, :], ps[b][0:1, 0:MODP], [("ps", b)], [("modrow",)])

    def mod_transpose(i, cb):
        b2 = nb()
        nchunk = MODP // 128
        for q in range(nchunk):
            mm(ps[b2][:, q:q + 1], modrow[0:1, q * 128:(q + 1) * 128], one11[0:1, 0:1], True, True,
               [("modrow",), ("one11",)], [("ps", b2)])
        c0 = cb * nchunk
        tt("dve", modT[:, i, c0:c0 + nchunk], ps[b2][:, 0:nchunk], V(f"modb{i}", nchunk, c0), ALU.add,
           [("ps", b2), ("vecs",)], [("modT", i, cb)])

    PIECES_PER_8 = 8 * 128 // MODP

    def mod_finish(i, part):
        lo, hi = [(0, 16), (16, 24), (24, 40), (40, 48)][part]
        rd = [("modT", i, cb) for cb in range(lo * 128 // MODP, hi * 128 // MODP)] + [("vecs",)]
        wr = [("lvec", i, part)]
        if part == 0:
            stt(lvec[:, i, 0:8], modT[:, i, 8:16], 1.0, V(f"npm{i}", 8), ALU.add, ALU.mult, rd, wr)
            cp("dve", lvec[:, i, 8:16], modT[:, i, 0:8], rd, wr)
        elif part == 1:
            tt("dve", lvec[:, i, 16:24], modT[:, i, 16:24], V(f"nqm{i}", 8), ALU.mult, rd, wr)
        elif part == 2:
            stt(lvec[:, i, 24:32], modT[:, i, 32:40], 1.0, V(f"npf{i}", 8), ALU.add, ALU.mult, rd, wr)
            cp("dve", lvec[:, i, 32:40], modT[:, i, 24:32], rd, wr)
        else:
            tt("dve", lvec[:, i, 40:48], modT[:, i, 40:48], V(f"nqf{i}", 8), ALU.mult, rd, wr)

    bg_tasks = []

    def bg_step(k=1):
        for _ in range(k):
            if bg_tasks:
                bg_tasks.pop(0)()

    def queue_mod(i):
        ends = {16 * 128 // MODP - 1: 0, 24 * 128 // MODP - 1: 1, 40 * 128 // MODP - 1: 2, 48 * 128 // MODP - 1: 3}

        def fin(cb):
            bg_tasks.append(lambda i=i, cb=cb: mod_transpose(i, cb))
            if cb in ends:
                bg_tasks.append(lambda i=i, p=ends[cb]: (mod_finish(i, p), bg_done.add((i, p))))
        for cb in range(NMODP):
            bg_tasks.append(lambda i=i, cb=cb: mod_piece(i, cb, 0))
            if cb > 0:
                fin(cb - 1)
            bg_tasks.append(lambda i=i, cb=cb: mod_piece(i, cb, 1))
        fin(NMODP - 1)

    bg_done = set()

    def bg_until(tag):
        while bg_tasks and tag not in bg_done:
            bg_tasks.pop(0)()

    mp = []
    wp = []
    bg_rate = [1, 0, 0]

    def use_w(src2d, ncols, extra=()):
        k = len(wp)
        sl = k % NWS
        subs = [(0, src2d, ncols)]
        off_ = ncols
        for (s2, n2) in extra:
            subs.append((off_, s2, n2))
            off_ += n2
        assert off_ <= KC * 384
        wp.append((len(sch.ops), sl, subs))
        return wsl[sl], ("wslot", sl)

    def use_w_group(srcs, ncols):
        per = (KC * 384) // ncols
        res_ = []
        for a0 in range(0, len(srcs), per):
            grp = srcs[a0:a0 + per]
            t_, k_ = use_w(grp[0], ncols, [(s_, ncols) for s_ in grp[1:]])
            for gi in range(len(grp)):
                res_.append((t_, gi * ncols, k_))
        return res_

    def LV(i, which, kc):
        o = which * 8 + kc
        return lvec[:, i, o:o + 1]

    XK = [("x", kc, n) for kc in range(KC) for n in range(NSUB)]
    YK = [("y", kc, n) for kc in range(KC) for n in range(NSUB)]

    def dump(nm, src_ap, reads):
        if nm in dump_d:
            dma("sp", dump_d[nm], src_ap, reads, [("dump", nm)], "dump_" + nm)

    def arena_barrier():
        barrier((), ARENA_KEYS, eng="sp")

    def load_x(vc):
        arena_barrier()
        blocks = [(0, H)] + [(H + 128 * t, 128) for t in range(SV // 128)]
        for bi, (t0, nr) in enumerate(blocks):
            sl = bi % 2
            dma("sp", stg[sl][0:nr, :], xin[vc, t0:t0 + nr, :], (), [("stg", sl, 0), ("stg", sl, 1)], f"x{sl}")
            for half in range(2):
                b = nb()
                for q in range(4):
                    kc = half * 4 + q
                    sch.add("pe", lambda e, b=b, q=q, kc=kc, sl=sl, nr=nr: e.transpose(
                        ps[b][:, q * 128:q * 128 + nr], stg[sl][0:nr, kc * 128:(kc + 1) * 128], ident[0:nr, 0:nr]),
                        [("stg", sl, half), ("ident",)], [("ps", b)])
                n0 = t0 // NT
                n1 = (t0 + nr - 1) // NT
                wk = [("x", half * 4 + q, n) for q in range(4) for n in range(n0, n1 + 1)]
                src_ = ps[b][:, :].rearrange("p (q t) -> p q t", q=4)[:, :, 0:nr]
                cp("act" if half == 0 else "dve", xres[:, half * 4:half * 4 + 4, t0:t0 + nr], src_, [("ps", b)], wk)

    def store_out(vc):
        arena_barrier()
        for tb in range(SV // 128):
            sl = tb % 2
            c0 = H + tb * 128
            n0 = c0 // NT
            n1 = (c0 + 127) // NT
            for half in range(2):
                b = nb()
                rk = [("x", half * 4 + q, n) for q in range(4) for n in range(n0, n1 + 1)]
                for q in range(4):
                    kc = half * 4 + q
                    sch.add("pe", lambda e, b=b, q=q, kc=kc, c0=c0: e.transpose(
                        ps[b][:, q * 128:(q + 1) * 128], xres[:, kc, c0:c0 + 128], ident[:, :]),
                        rk + [("ident",)], [("ps", b)])
                cp("act" if half == 0 else "dve", stg[sl][:, half * 512:(half + 1) * 512], ps[b][:, :],
                   [("ps", b)], [("stg", sl, half)])
            dma("sp", out_d[vc, tb * 128:(tb + 1) * 128, :], stg[sl][:, :],
                [("stg", sl, 0), ("stg", sl, 1)], [("out", vc, tb)], f"o{sl}")

    def rms_stats(src, skey, n, par):
        c0 = n * NT
        rk = [(skey, kc, n) for kc in range(KC)]
        sch.add("act", lambda e: e.activation(out=sqb[par][:, :, :], in_=src[:, :, c0:c0 + NT], func=AF.Square),
                rk, [("sqb", par, 0), ("sqb", par, 1)])
        b = nb()
        for kc in range(KC):
            mm(ps[b][:, 0:NT], onesm[:, :], sqb[par][:, kc, :], kc == 0, kc == KC - 1,
               [("sqb", par, kc // 4), ("onesm",)], [("ps", b)])
        act(sd_t[par][:, :], ps[b][:, 0:NT], AF.Sqrt, [("ps", b), ("epsr",)], [("sd", par)], bias=epsr[:, 0:1])
        sch.add("dve", lambda e: e.reciprocal(out=rstd_t[par][:, :], in_=sd_t[par][:, :]),
                [("sd", par)], [("rstd", par)])

    def pre_norm(i, which, n, par, dst, dkey, off):
        rms_stats(xres, "x", n, par)
        c0 = n * NT
        for kc in range(KC):
            t = ntmp()
            tt("dve" if kc % 2 == 0 else PN_ENG, tmpf[t][:, 0:NT], xres[:, kc, c0:c0 + NT], rstd_t[par][:, :], ALU.mult,
               [("x", kc, n), ("rstd", par)], [("tmp", t)])
            act(dst[:, kc, off + c0:off + c0 + NT], tmpf[t][:, 0:NT], AF.Identity,
                [("tmp", t), ("lvec", i, which // 3 * 2)], [(dkey, kc, n)],
                scale=LV(i, which, kc), bias=LV(i, which + 1, kc))
        if n == 0:
            for kc in range(KC):
                tt(MASK_ENG, dst[:, kc, off:off + H], dst[:, kc, off:off + H], maskt[:, LM:LM + H], ALU.mult,
                   [(dkey, kc, 0), ("mask",)], [(dkey, kc, 0)])

    def post_norm(i, which, n, par):
        rms_stats(ybuf, "y", n, par)
        c0 = n * NT
        tl = {}

        def p1(kc):
            tl[kc] = ntmp()
            tt(PN_ENG, tmpf[tl[kc]][:, 0:NT], ybuf[:, kc, c0:c0 + NT], rstd_t[par][:, :], ALU.mult,
               [("y", kc, n), ("rstd", par)], [("tmp", tl[kc])])

        def p2(kc):
            t = tl[kc]
            stt(xres[:, kc, c0:c0 + NT], tmpf[t][:, 0:NT], LV(i, which, kc), xres[:, kc, c0:c0 + NT],
                ALU.mult, ALU.add, [("tmp", t), ("lvec", i, which // 3 * 2 + 1), ("x", kc, n)], [("x", kc, n)])
        p1(0)
        p1(1)
        for kc in range(KC):
            p2(kc)
            if kc + 2 < KC:
                p1(kc + 2)

    def run_pipelined(stage, cbs):
        post, pre = cbs
        stage(0)
        stage(1)
        post(0, 0)
        bg_step(bg_rate[2])
        stage(2)
        post(1, 1)
        bg_step(bg_rate[2])
        pre(0, 0)
        bg_step(bg_rate[2])
        post(2, 1)
        bg_step(bg_rate[2])
        pre(1, 0)
        bg_step(bg_rate[2])
        pre(2, 1)
        bg_step(bg_rate[2])

    def hkeys(key, kc, n, E):
        ks = [(key, kc, n)]
        if E > 0:
            ks.append((key, kc, n - 1) if n > 0 else ("hm",))
        return ks

    def ffn(i, vc, after_y):
        arena_barrier()
        E = 2
        fdw = VOFF[f"fdw{i}"]
        fdb = VOFF[f"fdb{i}"]
        pend = []
        for hh in range(2):
            for jj in range(NHP):
                j = hh * NHP + jj
                wt, wk = use_w(wup[i, j], 2048)
                for n in range(NSUB):
                    c0 = n * NT
                    bg_, bv_ = nb(), nb()
                    for half, b in ((0, bg_), (1, bv_)):
                        for kc in range(KC):
                            mm(ps[b][:, 0:NT + E],
                               wt[:, kc * 256 + half * 128: kc * 256 + half * 128 + 128],
                               hb[:, kc, LM + c0 - E: LM + c0 + NT], kc == 0, kc == KC - 1,
                               [wk] + hkeys("h", kc, n, E), [("ps", b)])
                    res = []
                    hb_ = ((0, bg_), (1, bv_))
                    for half, b in hb_:
                        ch = half * NPAIR + j
                        t = ntmp()
                        res.append(t)
                        act(tmpf[t][:, 0:NT], ps[b][:, 2:NT + 2], AF.Identity, [("ps", b), ("vecs",)], [("tmp", t)],
                            scale=vecs[:, fdw + 2 * 44 + ch: fdw + 2 * 44 + ch + 1], bias=vecs[:, fdb + ch: fdb + ch + 1])
                    for tap in (1, 0):
                        for half, b in hb_:
                            ch = half * NPAIR + j
                            t = res[half]
                            stt(tmpf[t][:, 0:NT], ps[b][:, tap:NT + tap], vecs[:, fdw + tap * 44 + ch: fdw + tap * 44 + ch + 1],
                                tmpf[t][:, 0:NT], ALU.mult, ALU.add, [("ps", b), ("vecs",), ("tmp", t)], [("tmp", t)])
                    tg, tv = res
                    t2 = ntmp()
                    act(tmpf[t2][:, 0:NT], tmpf[tg][:, 0:NT], AF.Silu, [("tmp", tg)], [("tmp", t2)])
                    if pend:
                        pend.pop(0)()
                    pend.append(lambda jj=jj, c0=c0, t2=t2, tv=tv, n=n: tt(
                        GM_ENG, gbuf[:, jj, c0:c0 + NT], tmpf[t2][:, 0:NT], tmpf[tv][:, 0:NT], ALU.mult,
                        [("tmp", t2), ("tmp", tv)], [("g", jj, n)]))
                    bg_step(bg_rate[0])
            while pend:
                pend.pop(0)()
            if hh == 1:
                wg = use_w_group([wdn[i, m][:, NHP * 128:2 * NHP * 128] for m in range(KC)], NHP * 128)

                def down(n):
                    c0 = n * NT
                    for m in range(KC):
                        wt, wo, wk = wg[m]
                        b = nb()
                        for jj in range(NHP):
                            mm(ps[b][:, 0:NT], wt[:, wo + jj * 128: wo + (jj + 1) * 128], gbuf[:, jj, c0:c0 + NT],
                               jj == 0, jj == NHP - 1, [wk, ("g", jj, n)], [("ps", b)])
                        tt("dve", ybuf[:, m, c0:c0 + NT], ybuf[:, m, c0:c0 + NT], ps[b][:, 0:NT], ALU.add,
                           [("ps", b), ("y", m, n)], [("y", m, n)])
                run_pipelined(down, after_y)
                continue
            for m in range(KC):
                wt, wk = use_w(wdn[i, m][:, hh * NHP * 128:(hh + 1) * NHP * 128], NHP * 128)
                for n in range(NSUB):
                    c0 = n * NT
                    b = nb()
                    for jj in range(NHP):
                        mm(ps[b][:, 0:NT], wt[:, jj * 128:(jj + 1) * 128], gbuf[:, jj, c0:c0 + NT],
                           jj == 0, jj == NHP - 1, [wk, ("g", jj, n)], [("ps", b)])
                    if hh == 0:
                        cp("act", ybuf[:, m, c0:c0 + NT], ps[b][:, 0:NT], [("ps", b)], [("y", m, n)])
                    else:
                        tt("dve", ybuf[:, m, c0:c0 + NT], ybuf[:, m, c0:c0 + NT], ps[b][:, 0:NT], ALU.add,
                           [("ps", b), ("y", m, n)], [("y", m, n)])
        dump(f"yffn{i}", ybuf[:], YK)
        dump(f"xffn{i}", xres[:], XK)

    def mixer_b(i, vc, after_y):
        dump(f"hmix{i}", hf[:], YK)

        def pool(n):
            c0 = n * NT
            E = 15 if n > 0 else 0
            W = NT + E
            for kc in range(KC):
                g = kc // 2
                ta, tb_ = ntmp(), ntmp()
                s0 = hf[:, kc, c0 - E: c0 + NT]
                rk = hkeys("y", kc, n, E)
                A_ = tmpf[ta]
                B_ = tmpf[tb_]
                tt("dve", A_[:, 1:W], s0[:, 1:W], s0[:, 0:W - 1], ALU.add, rk, [("tmp", ta)])
                cur, curk = A_, ta
                if g >= 1:
                    tt("dve", B_[:, 3:W], A_[:, 3:W], A_[:, 1:W - 2], ALU.add, [("tmp", ta)], [("tmp", tb_)])
                    cur, curk = B_, tb_
                if g >= 2:
                    tt("dve", A_[:, 7:W], B_[:, 7:W], B_[:, 3:W - 4], ALU.add, [("tmp", tb_), ("tmp", ta)], [("tmp", ta)])
                    cur, curk = A_, ta
                if g >= 3:
                    tt("dve", B_[:, 15:W], A_[:, 15:W], A_[:, 7:W - 8], ALU.add, [("tmp", ta), ("tmp", tb_)], [("tmp", tb_)])
                    cur, curk = B_, tb_
                wv = float(2 ** (g + 1))
                lo = 15 - E
                stt(hb[:, kc, LM + c0 + lo: LM + c0 + NT], cur[:, E + lo:W], 1.0 / wv, hf[:, kc, c0 + lo:c0 + NT],
                    ALU.mult, ALU.subtract, [("tmp", curk), ("y", kc, n)], [("h", kc, n)])
                if n == 0:
                    memset("dve", hb[:, kc, LM:LM + lo], 0.0, [("h", kc, 0)])
                    t3 = ntmp()
                    tt("dve", tmpf[t3][:, 0:16], cur[:, H:H + 16], invct[:, g * 16:(g + 1) * 16], ALU.mult,
                       [("tmp", curk), ("invc",)], [("tmp", t3)])
                    tt("dve", hb[:, kc, LM + H:LM + H + 16], tmpf[t3][:, 0:16], hf[:, kc, H:H + 16],
                       ALU.subtract, [("tmp", t3), ("y", kc, 0), ("h", kc, 0)], [("h", kc, 0)])

        def gmm(n):
            c0 = n * NT
            for m in range(KC):
                g = m // 2
                b = nb()
                for kk in range(2):
                    kc = 2 * g + kk
                    o = g * 512 + kk * 256 + (m % 2) * 128
                    mm(ps[b][:, 0:NT], bgws[:, o:o + 128], hb[:, kc, LM + c0:LM + c0 + NT], kk == 0, kk == 1,
                       [("bgws",), ("h", kc, n)], [("ps", b)])
                act(ybuf[:, m, c0:c0 + NT], ps[b][:, 0:NT], AF.Identity, [("ps", b), ("vecs",), ("bsv",)], [("y", m, n)],
                    scale=V("bsc", 1, m), bias=bsv[:, m:m + 1])

        pool(0)

        def stage(n):
            if n + 1 < NSUB:
                pool(n + 1)
            gmm(n)
        run_pipelined(stage, after_y)

    def mixer_c(i, vc, after_y):
        arena_barrier()
        E = 2
        ccw = VOFF["ccw"]
        for m in range(KC):
            wt, wk = use_w(cwin[m], 3072)
            for n in range(NSUB):
                c0 = n * NT
                bs = [nb(), nb(), nb()]
                for t3 in range(3):
                    for kc in range(KC):
                        mm(ps[bs[t3]][:, 0:NT + E], wt[:, kc * 384 + t3 * 128: kc * 384 + t3 * 128 + 128],
                           hb[:, kc, LM + c0 - E: LM + c0 + NT], kc == 0, kc == KC - 1,
                           [wk] + hkeys("h", kc, n, E), [("ps", bs[t3])])
                tv, tz, tc = ntmp(), ntmp(), ntmp()
                cp("act", tmpf[tv][:, 0:NT + E], ps[bs[2]][:, 0:NT + E], [("ps", bs[2])], [("tmp", tv)])
                tt("dve", tmpf[tz][:, 0:NT + E], ps[bs[1]][:, 0:NT + E], tmpf[tv][:, 0:NT + E], ALU.mult,
                   [("ps", bs[1]), ("tmp", tv)], [("tmp", tz)])
                w = lambda k, m=m: vecs[:, ccw + k * 8 + m: ccw + k * 8 + m + 1]
                act(tmpf[tc][:, 0:NT], tmpf[tz][:, 2:NT + 2], AF.Identity, [("tmp", tz), ("vecs",)], [("tmp", tc)],
                    scale=w(2))
                stt(tmpf[tc][:, 0:NT], tmpf[tz][:, 1:NT + 1], w(1), tmpf[tc][:, 0:NT], ALU.mult, ALU.add,
                    [("tmp", tz), ("vecs",), ("tmp", tc)], [("tmp", tc)])
                stt(tmpf[tc][:, 0:NT], tmpf[tz][:, 0:NT], w(0), tmpf[tc][:, 0:NT], ALU.mult, ALU.add,
                    [("tmp", tz), ("vecs",), ("tmp", tc)], [("tmp", tc)])
                tt("dve", rbuf[:, m, c0:c0 + NT], ps[bs[0]][:, 2:NT + 2], tmpf[tc][:, 0:NT], ALU.mult,
                   [("ps", bs[0]), ("tmp", tc)], [("r", m, n)])
        wg = use_w_group([cwout[m] for m in range(KC)], 1024)
        def stage(n):
            c0 = n * NT
            for m in range(KC):
                wt, wo, wk = wg[m]
                b = nb()
                for kc in range(KC):
                    mm(ps[b][:, 0:NT], wt[:, wo + kc * 128: wo + (kc + 1) * 128], rbuf[:, kc, c0:c0 + NT],
                       kc == 0, kc == KC - 1, [wk, ("r", kc, n)], [("ps", b)])
                cp("act", ybuf[:, m, c0:c0 + NT], ps[b][:, 0:NT], [("ps", b)], [("y", m, n)])
        run_pipelined(stage, after_y)

    def mixer_a(i, vc, after_y):
        slot = i // 3
        arena_barrier()
        E = 30
        ab1 = VOFF[f"ab1{slot}"]
        adw = VOFF[f"adw{slot}"]
        units = [(m, n) for m in range(KC) for n in range(NSUB)]
        state = {}

        def emit_pw1(idx):
            m, n = units[idx]
            if n == 0:
                state["w"] = use_w(aw1[slot, m], 2048)
                dpar = m % 2
                for k in range(31):
                    ts("dve", diag[dpar][:, k, :], identb[:, :], vecs[:, adw + k * 8 + m: adw + k * 8 + m + 1], None,
                       ALU.mult, ALU.bypass, [("identb",), ("vecs",)], [("diag", dpar)])
            wt, wk = state["w"]
            c0 = n * NT
            ba, bg2 = nb(), nb()
            for half, b_ in ((0, ba), (1, bg2)):
                for kc in range(KC):
                    mm(ps[b_][:, 0:NT + E],
                       wt[:, kc * 256 + half * 128: kc * 256 + half * 128 + 128],
                       hb[:, kc, LM + c0 - E: LM + c0 + NT], kc == 0, kc == KC - 1,
                       [wk] + hkeys("h", kc, n, E), [("ps", b_)])
            gp = idx % 2
            t = ntmp()
            act(tmpf[t][:, 0:NT + E], ps[bg2][:, 0:NT + E], AF.Sigmoid, [("ps", bg2), ("vecs",)], [("tmp", t)],
                bias=vecs[:, ab1 + 8 + m: ab1 + 8 + m + 1])
            stt(glu[gp][:, 0:NT + E], ps[ba][:, 0:NT + E], vecs[:, ab1 + m: ab1 + m + 1], tmpf[t][:, 0:NT + E],
                ALU.add, ALU.mult, [("ps", ba), ("vecs",), ("tmp", t)], [("glu", gp)])
            if n == 0:
                tt(MASK_ENG, glu[gp][:, 0:E + H], glu[gp][:, 0:E + H], maskt[:, LM - E:LM + H], ALU.mult,
                   [("glu", gp), ("mask",)], [("glu", gp)])
            bg_step(bg_rate[1])

        def emit_conv(idx):
            m, n = units[idx]
            c0 = n * NT
            gp = idx % 2
            dpar = m % 2
            bc = nb()
            for k in range(31):
                mm(ps[bc][:, 0:NT], diag[dpar][:, k, :], glu[gp][:, k:k + NT], k == 0, k == 30,
                   [("diag", dpar), ("glu", gp)], [("ps", bc)])
            act(u2[:, m, c0:c0 + NT], ps[bc][:, 0:NT], AF.Identity, [("ps", bc), ("vecs",)], [("y", m, n)],
                bias=V(f"adb{slot}", 1, m))

        for idx in range(len(units) + 1):
            if idx < len(units):
                emit_pw1(idx)
            if idx >= 1:
                emit_conv(idx - 1)
        dump(f"aconv{i}", u2[:], YK)
        wg = use_w_group([aw2[slot, m] for m in range(KC)], 1024)
        def ln_stats(n):
            c0 = n * NT
            par = n % 2
            rk = [("y", kc, n) for kc in range(KC)]
            sch.add("act", lambda e, c0=c0, par=par: e.activation(out=sqb[par][:, :, :], in_=u2[:, :, c0:c0 + NT],
                                                                  func=AF.Square), rk, [("sqb", par, 0), ("sqb", par, 1)])
            cp("dve", sqb[1 - par][:, :, :], u2[:, :, c0:c0 + NT], rk, [("sqb", 1 - par, 0), ("sqb", 1 - par, 1)])
            bm, be = nb(), nb()
            for kc in range(KC):
                mm(ps[bm][:, 0:NT], onesm[:, :], sqb[1 - par][:, kc, :], kc == 0, kc == KC - 1,
                   [("sqb", 1 - par, 0), ("sqb", 1 - par, 1), ("onesm",)], [("ps", bm)])
            for kc in range(KC):
                mm(ps[be][:, 0:NT], onesm[:, :], sqb[par][:, kc, :], kc == 0, kc == KC - 1,
                   [("sqb", par, 0), ("sqb", par, 1), ("onesm",)], [("ps", be)])
            t = ntmp()
            act(tmpf[t][:, 0:NT], ps[bm][:, 0:NT], AF.Square, [("ps", bm)], [("tmp", t)])
            tt("dve", tmpf[t][:, 0:NT], ps[be][:, 0:NT], tmpf[t][:, 0:NT], ALU.subtract,
               [("ps", be), ("tmp", t)], [("tmp", t)])
            ts("dve", tmpf[t][:, 0:NT], tmpf[t][:, 0:NT], 0.0, None, ALU.max, ALU.bypass, [("tmp", t)], [("tmp", t)])
            act(sd_t[par][:, :], tmpf[t][:, 0:NT], AF.Sqrt, [("tmp", t), ("epsl",)], [("sd", par)], bias=epsl[:, 0:1])
            sch.add("dve", lambda e, par=par: e.reciprocal(out=rstd_t[par][:, :], in_=sd_t[par][:, :]),
                    [("sd", par)], [("rstd", par)])
            stt(nmr_t[par][:, :], ps[bm][:, 0:NT], -1.0, rstd_t[par][:, :], ALU.mult, ALU.mult,
                [("ps", bm), ("rstd", par)], [("nmr", par)])

        def ln_apply(n):
            c0 = n * NT
            par = n % 2
            tl = {}

            def q1(kc):
                tl[kc] = ntmp()
                tt("dve", tmpf[tl[kc]][:, 0:NT], u2[:, kc, c0:c0 + NT], rstd_t[par][:, :], ALU.mult,
                   [("y", kc, n), ("rstd", par)], [("tmp", tl[kc])])

            def q2(kc):
                t1 = tl[kc]
                tt(PN_ENG, tmpf[t1][:, 0:NT], tmpf[t1][:, 0:NT], nmr_t[par][:, :], ALU.add,
                   [("tmp", t1), ("nmr", par)], [("tmp", t1)])
                act(hb[:, kc, LM + c0:LM + c0 + NT], tmpf[t1][:, 0:NT], AF.Silu, [("tmp", t1), ("vecs",)],
                    [("h", kc, n)], scale=V(f"alg{slot}", 1, kc), bias=V(f"alb{slot}", 1, kc))
            q1(0)
            q1(1)
            for kc in range(KC):
                q2(kc)
                if kc + 2 < KC:
                    q1(kc + 2)

        def pw2(n):
            c0 = n * NT
            for m in range(KC):
                wt, wo, wk = wg[m]
                b = nb()
                for kc in range(KC):
                    mm(ps[b][:, 0:NT], wt[:, wo + kc * 128: wo + (kc + 1) * 128], hb[:, kc, LM + c0:LM + c0 + NT],
                       kc == 0, kc == KC - 1, [wk, ("h", kc, n)], [("ps", b)])
                act(ybuf[:, m, c0:c0 + NT], ps[b][:, 0:NT], AF.Identity, [("ps", b), ("vecs",)], [("y", m, n)],
                    bias=V(f"ab2{slot}", 1, m))

        ln_stats(0)
        ln_apply(0)

        def stage(n):
            if n + 1 < NSUB:
                ln_stats(n + 1)
            pw2(n)
            if n + 1 < NSUB:
                ln_apply(n + 1)
        run_pipelined(stage, after_y)

    for t_ in range(NTMP):
        memset("dve", tmpf[t_][:], 0.0, [("tmp", t_)])

    def pre_norm_mixer(i, n, par):
        if i % 3 == 1:
            pre_norm(i, 0, n, par, hf, "y", 0)
        else:
            pre_norm(i, 0, n, par, hb, "h", LM)

    for vc in range(cfg.nvc):
        nbanks[0] = 7 if vc == 0 else 8
        dma("sp", maskt[:], mask_d[vc], (), [("mask",)], "mk")
        dma("sp", invct[:], invc_d[vc], (), [("invc",)], "iv")
        load_x(vc)
        if vc == 0:
            queue_mod(0)
            bg_until((0, 0))
        dump("x0", xres[:], XK)
        for n in range(NSUB):
            pre_norm_mixer(0, n, n % 2)
        for i in range(cfg.nlayers):
            if vc == 0 and i + 1 < cfg.nlayers:
                queue_mod(i + 1)
                if not cfg.mod_bg:
                    bg_step(10 ** 6)
            bg_rate[0] = 0
            bg_rate[1] = 3 if (vc == 0 and i == 0) else 0
            bg_rate[2] = 7 if vc == 0 else 0

            after_mix = (lambda n, par, i=i: post_norm(i, 2, n, par),
                         lambda n, par, i=i: pre_norm(i, 3, n, par, hb, "h", LM))
            after_ffn = (lambda n, par, i=i: post_norm(i, 5, n, par),
                         lambda n, par, i=i: (pre_norm_mixer(i + 1, n, par) if i + 1 < cfg.nlayers else None))
            kind = i % 3
            if kind == 0:
                mixer_a(i, vc, after_mix)
            elif kind == 1:
                mixer_b(i, vc, after_mix)
            else:
                mixer_c(i, vc, after_mix)
            dump(f"xmix{i}", xres[:], XK)
            ffn(i, vc, after_ffn)
            bg_step(10 ** 6)
        store_out(vc)
    final_reads = [("out", vc, tb) for vc in range(cfg.nvc) for tb in range(SV // 128)] + \
                  [("dump", nm) for nm in cfg.dumps]
    sch.add("sp", None, final_reads, ())

    ins = {}

    def mk_dma(out, in_, writes, key):
        op = Op("pool", lambda e: e.dma_start(out=out, in_=in_), (), tuple(writes), key)
        return op
    def last_uses(pieces, keyname, nslots):
        firsts = [p[0] for p in pieces]
        owner = {}
        nxt = 0
        lu = [p[0] for p in pieces]
        for idx, op in enumerate(sch.ops):
            while nxt < len(pieces) and firsts[nxt] <= idx:
                owner[pieces[nxt][1]] = nxt
                nxt += 1
            for kk in op.reads:
                if kk[0] == keyname and kk[1] in owner:
                    lu[owner[kk[1]]] = idx
        return lu

    wlu = last_uses(wp, "wslot", NWS)
    mlu = last_uses(mp, "mslot", MSLOT)
    for k, (pos, sl, subs) in enumerate(wp):
        at = wp[k - WDEPTH][0] if k >= WDEPTH else 0
        if k >= NWS:
            at = max(at, wlu[k - NWS] + 1)
        for (off_, src2d, ncols) in subs:
            nsp = (ncols + 2047) // 2048
            step = (ncols + nsp - 1) // nsp
            a_ = 0
            while a_ < ncols:
                b_ = min(ncols, a_ + step)
                ins.setdefault(at, []).append(mk_dma(wsl[sl][:, off_ + a_:off_ + b_], src2d[:, a_:b_],
                                                     [("wslot", sl)], f"w{sl}"))
                a_ = b_
    for q, (pos, sl, src2d) in enumerate(mp):
        at = mp[q - MDEPTH][0] if q >= MDEPTH else 0
        if q >= MSLOT:
            at = max(at, mlu[q - MSLOT] + 1)
        ins.setdefault(at, []).append(mk_dma(mslot[sl][:, :], src2d, [("mslot", sl)], f"m{sl}"))
    new_ops = []
    for idx, op in enumerate(sch.ops):
        if idx in ins:
            new_ops.extend(ins[idx])
        new_ops.append(op)
    sch.ops = new_ops
    for idx, op in enumerate(sch.ops):
        op.idx = idx

    sch.analyze()
    streams = sorted({sch.stream_of(op) for op in sch.ops if op.signal}, key=str)
    sems = {}
    sctx = []
    for st in streams:
        g = nc.semaphore("s_" + (st if isinstance(st, str) else "dma_" + st[1]))
        sems[st] = g.__enter__()
        sctx.append(g)
    blk = nc.Block()
    block = blk.__enter__()
    sch.emit_all(nc, block, sems)
    blk.__exit__(None, None, None)
    for g in reversed(sctx):
        g.__exit__(None, None, None)
    for g in reversed(ctx):
        g.__exit__(None, None, None)
    return nc, sch


def prep_shared(inp):
    f = np.float32
    sh = {}
    up = np.asarray(inp["f_up_w"], f).reshape(DEPTH, KC, 128, 2, NPAIR, 128)
    sh["wup"] = np.ascontiguousarray(up.transpose(0, 4, 2, 1, 3, 5)).reshape(DEPTH, NPAIR, 128, 2048)
    dn = np.asarray(inp["f_down_w"], f).reshape(DEPTH, NPAIR, 128, KC, 128)
    sh["wdn"] = np.ascontiguousarray(dn.transpose(0, 3, 2, 1, 4)).reshape(DEPTH, KC, 128, NPAIR * 128)
    w1 = np.asarray(inp["a_pw1_w"], f).reshape(2, KC, 128, 2, KC, 128)
    sh["aw1"] = np.ascontiguousarray(w1.transpose(0, 4, 2, 1, 3, 5)).reshape(2, KC, 128, 2048)
    w2 = np.asarray(inp["a_pw2_w"], f).reshape(2, KC, 128, KC, 128)
    sh["aw2"] = np.ascontiguousarray(w2.transpose(0, 3, 2, 1, 4)).reshape(2, KC, 128, 1024)
    ci = np.asarray(inp["c_in_w"], f)[0].reshape(KC, 128, 3, KC, 128)
    sh["cwin"] = np.ascontiguousarray(ci.transpose(3, 1, 0, 2, 4)).reshape(KC, 128, 3072)
    co = np.asarray(inp["c_out_w"], f)[0].reshape(KC, 128, KC, 128)
    sh["cwout"] = np.ascontiguousarray(co.transpose(2, 1, 0, 3)).reshape(KC, 128, 1024)
    gw = np.asarray(inp["b_group_w"], f)[0].reshape(4, 2, 128, 256)
    sh["bgw"] = np.ascontiguousarray(gw.transpose(2, 0, 1, 3)).reshape(128, 2048)
    mw = np.asarray(inp["mod_w"], f).reshape(DEPTH, KC, 128, NMODP, MODP)
    sh["modw"] = np.ascontiguousarray(mw.transpose(0, 3, 2, 1, 4)).reshape(DEPTH, NMODP, 128, KC * MODP)
    vec = np.zeros((128, NV), f)

    def put(name, arr):
        a = np.asarray(arr, f)
        vec[:, VOFF[name]:VOFF[name] + a.shape[1]] = a
    for i in range(DEPTH):
        put(f"npm{i}", _pk(inp["norm_pre_mix"][i]))
        put(f"nqm{i}", _pk(inp["norm_post_mix"][i]))
        put(f"npf{i}", _pk(inp["norm_pre_ffn"][i]))
        put(f"nqf{i}", _pk(inp["norm_post_ffn"][i]))
        put(f"modb{i}", _pk(inp["mod_b"][i]))
        fd = np.asarray(inp["f_dw_w"][i], f)
        put(f"fdw{i}", np.concatenate([_pk(fd[k]) for k in range(3)], axis=1))
        put(f"fdb{i}", _pk(inp["f_dw_b"][i]))
    for s in range(2):
        put(f"ab1{s}", _pk(inp["a_pw1_b"][s]))
        ad = np.asarray(inp["a_dw_w"][s], f)
        put(f"adw{s}", np.concatenate([_pk(ad[k]) for k in range(31)], axis=1))
        put(f"adb{s}", _pk(inp["a_dw_b"][s]))
        put(f"alg{s}", _pk(inp["a_ln_g"][s]))
        put(f"alb{s}", _pk(inp["a_ln_b"][s]))
        put(f"ab2{s}", _pk(inp["a_pw2_b"][s]))
    put("bgb", _pk(inp["b_group_b"][0]))
    put("bsc", _pk(inp["b_scale"][0]))
    cc = np.asarray(inp["c_conv_w"][0], f)
    put("ccw", np.concatenate([_pk(cc[k]) for k in range(3)], axis=1))
    sh["vecs"] = vec
    sh["ident"] = np.eye(128, dtype=f)
    return sh


def prep_core(inp, core):
    f = np.float32
    b = core // 4
    q = core % 4
    x = np.asarray(inp["x"], f)
    xin = np.zeros((NVC, S, D), f)
    mask = np.ones((NVC, 128, LM + H), f)
    invc = np.zeros((NVC, 128, 64), f)
    for vc in range(NVC):
        p0 = q * 2048 + vc * SV
        lo = p0 - H
        if lo >= 0:
            xin[vc] = x[b, lo:p0 + SV]
        else:
            xin[vc, -lo:] = x[b, 0:p0 + SV]
            mask[vc] = 0.0
        for g, w in enumerate((2, 4, 8, 16)):
            pos = p0 + np.arange(16) + 1
            invc[vc, :, g * 16:(g + 1) * 16] = (1.0 / np.minimum(pos, w)).astype(f)[None, :]
    cvec = _pk(inp["c"][b])
    return {"xin": xin, "mask": mask, "invc": invc, "cvec": cvec}


_PROG = {}


def kernel(**inputs):
    if "full" not in _PROG:
        _PROG["full"] = build_program(Cfg())[0]
    nc = _PROG["full"]
    sh = prep_shared(inputs)
    in_maps = []
    for core in range(NCORES):
        m = dict(sh)
        m.update(prep_core(inputs, core))
        in_maps.append(m)
    res = run_bass_kernel_spmd(nc, in_maps, core_ids=list(range(NCORES)))
    out = np.zeros((BATCH, SEQ, D), np.float32)
    for core in range(NCORES):
        b = core // 4
        q = core % 4
        o = res.results[core]["out"]
        for vc in range(NVC):
            p0 = q * 2048 + vc * SV
            out[b, p0:p0 + SV] = o[vc]
    return out
```

```python
import numpy as np
import concourse.bass as bass
import concourse.mybir as mybir
from concourse.bass_utils import run_bass_kernel_spmd

F32 = mybir.dt.float32
BF16 = mybir.dt.bfloat16
F32R = mybir.dt.float32r
AF = mybir.ActivationFunctionType
ALU = mybir.AluOpType

D = 1024
KC = 8
FF = 2816
NPAIR = 22
DEPTH = 4
SEQ = 8192
BATCH = 2
NCORES = 8
NVC = 2
SV = 1024
H = 86
S = SV + H
NT = 370
NSUB = 3
LM = 32
RMS_EPS = 1e-6
LN_EPS = 1e-5
MODP = 256
NMODP = 6 * D // MODP
NHP = NPAIR // 2

SAME_ENGINE_SYNC = True
MASK_ENG = "dve"
PN_ENG = "dve"
GM_ENG = "pool"
NWS = 4
WDEPTH = 3


def _vec_layout():
    off = {}
    n = 0

    def put(name, w):
        nonlocal n
        off[name] = n
        n += w
    for i in range(DEPTH):
        for nm in ("npm", "nqm", "npf", "nqf"):
            put(f"{nm}{i}", 8)
        put(f"modb{i}", 48)
        put(f"fdw{i}", 3 * 44)
        put(f"fdb{i}", 44)
    for s in range(2):
        put(f"ab1{s}", 16)
        put(f"adw{s}", 31 * 8)
        put(f"adb{s}", 8)
        put(f"alg{s}", 8)
        put(f"alb{s}", 8)
        put(f"ab2{s}", 8)
    put("bgb", 8)
    put("bsc", 8)
    put("ccw", 24)
    return off, n


VOFF, NV = _vec_layout()


def _pk(v):
    v = np.asarray(v, np.float32)
    return np.ascontiguousarray(v.reshape(-1, 128).T)


class Op:
    __slots__ = ("eng", "emit", "reads", "writes", "dma", "waits", "signal", "count", "idx")

    def __init__(self, eng, emit, reads, writes, dma):
        self.eng = eng
        self.emit = emit
        self.reads = reads
        self.writes = writes
        self.dma = dma
        self.waits = []
        self.signal = False
        self.count = 0


class Sched:
    ENGS = ("pe", "act", "dve", "pool", "sp")

    def __init__(self):
        self.ops = []

    def add(self, eng, emit, reads=(), writes=(), dma=None):
        op = Op(eng, emit, tuple(reads), tuple(writes), dma)
        op.idx = len(self.ops)
        self.ops.append(op)
        return op

    def stream_of(self, op):
        return ("dma", op.dma) if op.dma is not None else op.eng

    def analyze(self):
        last_w = {}
        readers = {}
        waited = {e: {} for e in self.ENGS}
        for i, op in enumerate(self.ops):
            deps = set()
            for k in op.reads:
                if k in last_w:
                    deps.add(last_w[k])
            for k in op.writes:
                if k in last_w:
                    deps.add(last_w[k])
                r = readers.get(k)
                if r:
                    deps.update(r)
            deps.discard(i)
            best = {}
            for d in deps:
                dop = self.ops[d]
                st = self.stream_of(dop)
                if dop.dma is None and dop.eng == op.eng and op.dma is None:
                    if op.eng == "pe" or not SAME_ENGINE_SYNC:
                        continue
                if st not in best or best[st] < d:
                    best[st] = d
            w = waited[op.eng]
            for st, d in best.items():
                if w.get(st, -1) >= d:
                    continue
                w[st] = d
                op.waits.append(d)
                self.ops[d].signal = True
            for k in op.reads:
                readers.setdefault(k, []).append(i)
            for k in op.writes:
                last_w[k] = i
                readers[k] = []
        cnt = {}
        for op in self.ops:
            st = self.stream_of(op)
            if op.dma is not None:
                op.signal = True
                cnt[st] = cnt.get(st, 0) + 16
                op.count = cnt[st]
            elif op.signal:
                cnt[st] = cnt.get(st, 0) + 1
                op.count = cnt[st]

    def emit_all(self, nc, block, sems):
        per = {e: [] for e in self.ENGS}
        for op in self.ops:
            per[op.eng].append(op)

        def run(eng_name, e):
            for op in per[eng_name]:
                for d in op.waits:
                    dop = self.ops[d]
                    e.wait_ge(sems[self.stream_of(dop)], dop.count)
                if op.emit is None:
                    continue
                ins = op.emit(e)
                if op.signal:
                    ins.then_inc(sems[self.stream_of(op)], 16 if op.dma is not None else 1)

        @block.tensor
        def _(e):
            run("pe", e)

        @block.scalar
        def _(e):
            run("act", e)

        @block.vector
        def _(e):
            run("dve", e)

        @block.gpsimd
        def _(e):
            run("pool", e)

        @block.sync
        def _(e):
            run("sp", e)


class Cfg:
    def __init__(self, nvc=NVC, nlayers=DEPTH, dumps=(), stop_after=None, mod_bg=True):
        self.nvc = nvc
        self.nlayers = nlayers
        self.dumps = tuple(dumps)
        self.stop_after = stop_after
        self.mod_bg = mod_bg


def build_program(cfg):
    nc = bass.Bass("TRN2", target_bir_lowering=False)
    sch = Sched()
    dram = {}

    def din(name, shape, dt=F32):
        dram[name] = nc.dram_tensor(name, list(shape), dt, kind="ExternalInput").ap()
        return dram[name]

    xin = din("xin", [NVC, S, D])
    wup = din("wup", [DEPTH, NPAIR, 128, 2048])
    wdn = din("wdn", [DEPTH, KC, 128, NPAIR * 128])
    aw1 = din("aw1", [2, KC, 128, 2048])
    aw2 = din("aw2", [2, KC, 128, 1024])
    cwin = din("cwin", [KC, 128, 3072])
    cwout = din("cwout", [KC, 128, 1024])
    bgw = din("bgw", [128, 2048])
    modw = din("modw", [DEPTH, NMODP, 128, KC * MODP])
    vecs_d = din("vecs", [128, NV])
    cvec_d = din("cvec", [128, KC])
    ident_d = din("ident", [128, 128])
    mask_d = din("mask", [NVC, 128, LM + H])
    invc_d = din("invc", [NVC, 128, 64])
    out_d = nc.dram_tensor("out", [NVC, SV, D], F32, kind="ExternalOutput").ap()
    dump_d = {}
    for nm in cfg.dumps:
        dump_d[nm] = nc.dram_tensor("dump_" + nm, [128, KC, S], F32, kind="ExternalOutput").ap()

    ctx = []

    def sb(name, shape, dt):
        g = nc.sbuf_tensor(name, list(shape), dt)
        t = g.__enter__()
        ctx.append(g)
        return t

    xres = sb("xres", [128, KC, S], F32)
    ybuf = sb("ybuf", [128, KC, S], F32)
    hb = sb("hb", [128, KC, LM + S], BF16)
    ARENA = 6112
    arena = sb("arena", [128, ARENA], F32)
    wsl = [sb(f"wslot{i}", [128, KC * 384], BF16) for i in range(NWS)]
    MSLOT = 6
    MDEPTH = 5
    mslot = [sb(f"mslot{i}", [128, 4 * MODP], F32R) for i in range(MSLOT)]
    vecs = sb("vecs_sb", [128, NV], F32)
    lvec = sb("lvec", [128, DEPTH, 48], F32)
    modT = sb("modT", [128, DEPTH, 48], F32)
    cact = sb("cact", [128, KC], F32)
    cact_r = sb("cact_r", [128, KC], F32R)
    ident = sb("ident_sb", [128, 128], F32)
    identb = sb("identb", [128, 128], BF16)
    onesm = sb("onesm", [128, 128], BF16)
    one11 = sb("one11", [1, 2], F32)
    epsr = sb("epsr", [128, 1], F32)
    epsl = sb("epsl", [128, 1], F32)
    maskt = sb("maskt", [128, LM + H], F32)
    invct = sb("invct", [128, 64], F32)
    bgws = sb("bgws", [128, 2048], BF16)
    modrow = sb("modrow", [1, MODP], F32)
    bsv = sb("bsv", [128, KC], F32)
    NTMP = 8
    TW = NT + LM
    tmpf = [sb(f"tmpf{i}", [128, TW], F32) for i in range(NTMP)]
    sqb = [sb(f"sqb{i}", [128, KC, NT], BF16) for i in range(2)]
    rstd_t = [sb(f"rstd{i}", [128, NT], F32) for i in range(2)]
    nmr_t = [sb(f"nmr{i}", [128, NT], F32) for i in range(2)]
    sd_t = [sb(f"sd{i}", [128, NT], F32) for i in range(2)]

    gbuf = arena[:, 0:NHP * S // 2].bitcast(BF16).rearrange("p (j s) -> p j s", j=NHP)
    GLW = NT + 30
    glu = [arena[:, i * (GLW // 2):(i + 1) * (GLW // 2)].bitcast(BF16) for i in range(2)]
    DG_OFF = GLW
    DGW = 31 * 128 // 2
    diag = [arena[:, DG_OFF + i * DGW: DG_OFF + (i + 1) * DGW].bitcast(BF16).rearrange("p (k c) -> p k c", k=31)
            for i in range(2)]
    assert DG_OFF + 2 * DGW <= ARENA, (DG_OFF + 2 * DGW, ARENA)
    rbuf = arena[:, 0:KC * S // 2].bitcast(BF16).rearrange("p (k s) -> p k s", k=KC)
    stg = [arena[:, i * D:(i + 1) * D] for i in range(2)]
    assert NHP * S // 2 <= ARENA and KC * S // 2 <= ARENA
    u2 = ybuf
    hf = ybuf
    ARENA_KEYS = ([("g", j, n) for j in range(NHP) for n in range(NSUB)] + [("glu", 0), ("glu", 1), ("diag", 0),
                  ("diag", 1)] + [("r", k, n) for k in range(KC) for n in range(NSUB)] +
                  [("stg", s_, h_) for s_ in range(2) for h_ in range(2)])

    psg = [nc.psum_tensor(f"ps{i}", [128, 512], F32) for i in range(8)]
    ps = []
    for g in psg:
        ps.append(g.__enter__())
        ctx.append(g)
    bank_ctr = [0]
    nbanks = [7]

    def nb():
        b = bank_ctr[0] % nbanks[0]
        bank_ctr[0] += 1
        return b

    tmp_ctr = [0]

    def ntmp():
        t = tmp_ctr[0] % NTMP
        tmp_ctr[0] += 1
        return t

    def V(name, w=1, i=0):
        o = VOFF[name] + i
        return vecs[:, o:o + w]

    def dma(eng, out, in_, reads, writes, key):
        sch.add(eng, lambda e: e.dma_start(out=out, in_=in_), reads, writes, dma=key)

    def dma_cast(out2d, in2d, n, reads, writes, key):
        nsp = (n + 2047) // 2048
        step = (n + nsp - 1) // nsp
        a = 0
        while a < n:
            b = min(n, a + step)
            dma("pool", out2d[:, a:b], in2d[:, a:b], reads, writes, key)
            a = b

    def mm(out, lhsT, rhs, start, stop, reads, writes):
        sch.add("pe", lambda e: e.matmul(out, lhsT=lhsT, rhs=rhs, start=start, stop=stop), reads, writes)

    def act(out, in_, func, reads, writes, scale=1.0, bias=0.0):
        sch.add("act", lambda e: e.activation(out=out, in_=in_, func=func, bias=bias, scale=scale), reads, writes)

    def tt(eng, out, in0, in1, op, reads, writes):
        sch.add(eng, lambda e: e.tensor_tensor(out=out, in0=in0, in1=in1, op=op), reads, writes)

    def stt(out, in0, scalar, in1, op0, op1, reads, writes, eng="dve"):
        sch.add(eng, lambda e: e.scalar_tensor_tensor(out=out, in0=in0, scalar=scalar, in1=in1, op0=op0, op1=op1),
                reads, writes)

    def ts(eng, out, in0, s1, s2, op0, op1, reads, writes):
        sch.add(eng, lambda e: e.tensor_scalar(out=out, in0=in0, scalar1=s1, scalar2=s2, op0=op0, op1=op1),
                reads, writes)

    def cp(eng, out, in_, reads, writes):
        if eng == "act":
            sch.add("act", lambda e: e.copy(out=out, in_=in_), reads, writes)
        else:
            sch.add(eng, lambda e: e.tensor_copy(out=out, in_=in_), reads, writes)

    def memset(eng, ap, val, writes):
        sch.add(eng, lambda e: e.memset(ap, val), (), writes)

    def barrier(reads, writes, eng="pool"):
        sch.add(eng, lambda e: e.nop(), reads, writes)

    dma("sp", vecs[:], vecs_d, (), [("vecs",)], "c0")
    dma("sp", cact[:], cvec_d, (), [("cact",)], "c1")
    dma("sp", ident[:], ident_d, (), [("ident",)], "c2")
    dma_cast(bgws[:], bgw, 2048, (), [("bgws",)], "c3")
    memset("dve", onesm[:], 1.0 / 1024.0, [("onesm",)])
    memset("dve", one11[:], 1.0, [("one11",)])
    memset("dve", epsr[:], RMS_EPS, [("epsr",)])
    memset("dve", epsl[:], LN_EPS, [("epsl",)])
    memset("dve", hb[:, :, 0:LM], 0.0, [("hm",)])
    cp("dve", identb[:], ident[:], [("ident",)], [("identb",)])
    tt("dve", bsv[:, :], V("bgb", 8), V("bsc", 8), ALU.mult, [("vecs",)], [("bsv",)])
    act(cact_r[:], cact[:], AF.Silu, [("cact",)], [("cact_r",)])

    def mod_piece(i, cb, kh):
        sl = len(mp) % MSLOT
        kslot = ("mslot", sl)
        mp.append((len(sch.ops), sl, modw[i, cb][:, kh * 4 * MODP:(kh + 1) * 4 * MODP]))
        b = 7
        for q in range(4):
            kc = kh * 4 + q
            mm(ps[b][0:1, 0:MODP], cact_r[:, kc:kc + 1],
               mslot[sl][:, q * MODP:(q + 1) * MODP], kc == 0, kc == KC - 1,
               [kslot, ("cact_r",)], [("ps", b)])
        if kh == 1:
            cp("act", modrow[0:1, :], ps[b][0:1, 0:MODP], [("ps", b)], [("modrow",)])

    def mod_transpose(i, cb):
        b2 = nb()
        nchunk = MODP // 128
        for q in range(nchunk):
            mm(ps[b2][:, q:q + 1], modrow[0:1, q * 128:(q + 1) * 128], one11[0:1, 0:1], True, True,
               [("modrow",), ("one11",)], [("ps", b2)])
        c0 = cb * nchunk
        tt("dve", modT[:, i, c0:c0 + nchunk], ps[b2][:, 0:nchunk], V(f"modb{i}", nchunk, c0), ALU.add,
           [("ps", b2), ("vecs",)], [("modT", i, cb)])

    PIECES_PER_8 = 8 * 128 // MODP

    def mod_finish(i, part):
        lo, hi = [(0, 16), (16, 24), (24, 40), (40, 48)][part]
        rd = [("modT", i, cb) for cb in range(lo * 128 // MODP, hi * 128 // MODP)] + [("vecs",)]
        wr = [("lvec", i, part)]
        if part == 0:
            stt(lvec[:, i, 0:8], modT[:, i, 8:16], 1.0, V(f"npm{i}", 8), ALU.add, ALU.mult, rd, wr)
            cp("dve", lvec[:, i, 8:16], modT[:, i, 0:8], rd, wr)
        elif part == 1:
            tt("dve", lvec[:, i, 16:24], modT[:, i, 16:24], V(f"nqm{i}", 8), ALU.mult, rd, wr)
        elif part == 2:
            stt(lvec[:, i, 24:32], modT[:, i, 32:40], 1.0, V(f"npf{i}", 8), ALU.add, ALU.mult, rd, wr)
            cp("dve", lvec[:, i, 32:40], modT[:, i, 24:32], rd, wr)
        else:
            tt("dve", lvec[:, i, 40:48], modT[:, i, 40:48], V(f"nqf{i}", 8), ALU.mult, rd, wr)

    bg_tasks = []

    def bg_step(k=1):
        for _ in range(k):
            if bg_tasks:
                bg_tasks.pop(0)()

    def queue_mod(i):
        ends = {16 * 128 // MODP - 1: 0, 24 * 128 // MODP - 1: 1, 40 * 128 // MODP - 1: 2, 48 * 128 // MODP - 1: 3}

        def fin(cb):
            bg_tasks.append(lambda i=i, cb=cb: mod_transpose(i, cb))
            if cb in ends:
                bg_tasks.append(lambda i=i, p=ends[cb]: (mod_finish(i, p), bg_done.add((i, p))))
        for cb in range(NMODP):
            bg_tasks.append(lambda i=i, cb=cb: mod_piece(i, cb, 0))
            if cb > 0:
                fin(cb - 1)
            bg_tasks.append(lambda i=i, cb=cb: mod_piece(i, cb, 1))
        fin(NMODP - 1)

    bg_done = set()

    def bg_until(tag):
        while bg_tasks and tag not in bg_done:
            bg_tasks.pop(0)()

    mp = []
    wp = []
    bg_rate = [1, 0, 0]

    def use_w(src2d, ncols, extra=()):
        k = len(wp)
        sl = k % NWS
        subs = [(0, src2d, ncols)]
        off_ = ncols
        for (s2, n2) in extra:
            subs.append((off_, s2, n2))
            off_ += n2
        assert off_ <= KC * 384
        wp.append((len(sch.ops), sl, subs))
        return wsl[sl], ("wslot", sl)

    def use_w_group(srcs, ncols):
        per = (KC * 384) // ncols
        res_ = []
        for a0 in range(0, len(srcs), per):
            grp = srcs[a0:a0 + per]
            t_, k_ = use_w(grp[0], ncols, [(s_, ncols) for s_ in grp[1:]])
            for gi in range(len(grp)):
                res_.append((t_, gi * ncols, k_))
        return res_

    def LV(i, which, kc):
        o = which * 8 + kc
        return lvec[:, i, o:o + 1]

    XK = [("x", kc, n) for kc in range(KC) for n in range(NSUB)]
    YK = [("y", kc, n) for kc in range(KC) for n in range(NSUB)]

    def dump(nm, src_ap, reads):
        if nm in dump_d:
            dma("sp", dump_d[nm], src_ap, reads, [("dump", nm)], "dump_" + nm)

    def arena_barrier():
        barrier((), ARENA_KEYS, eng="sp")

    def load_x(vc):
        arena_barrier()
        blocks = [(0, H)] + [(H + 128 * t, 128) for t in range(SV // 128)]
        for bi, (t0, nr) in enumerate(blocks):
            sl = bi % 2
            dma("sp", stg[sl][0:nr, :], xin[vc, t0:t0 + nr, :], (), [("stg", sl, 0), ("stg", sl, 1)], f"x{sl}")
            for half in range(2):
                b = nb()
                for q in range(4):
                    kc = half * 4 + q
                    sch.add("pe", lambda e, b=b, q=q, kc=kc, sl=sl, nr=nr: e.transpose(
                        ps[b][:, q * 128:q * 128 + nr], stg[sl][0:nr, kc * 128:(kc + 1) * 128], ident[0:nr, 0:nr]),
                        [("stg", sl, half), ("ident",)], [("ps", b)])
                n0 = t0 // NT
                n1 = (t0 + nr - 1) // NT
                wk = [("x", half * 4 + q, n) for q in range(4) for n in range(n0, n1 + 1)]
                src_ = ps[b][:, :].rearrange("p (q t) -> p q t", q=4)[:, :, 0:nr]
                cp("act" if half == 0 else "dve", xres[:, half * 4:half * 4 + 4, t0:t0 + nr], src_, [("ps", b)], wk)

    def store_out(vc):
        arena_barrier()
        for tb in range(SV // 128):
            sl = tb % 2
            c0 = H + tb * 128
            n0 = c0 // NT
            n1 = (c0 + 127) // NT
            for half in range(2):
                b = nb()
                rk = [("x", half * 4 + q, n) for q in range(4) for n in range(n0, n1 + 1)]
                for q in range(4):
                    kc = half * 4 + q
                    sch.add("pe", lambda e, b=b, q=q, kc=kc, c0=c0: e.transpose(
                        ps[b][:, q * 128:(q + 1) * 128], xres[:, kc, c0:c0 + 128], ident[:, :]),
                        rk + [("ident",)], [("ps", b)])
                cp("act" if half == 0 else "dve", stg[sl][:, half * 512:(half + 1) * 512], ps[b][:, :],
                   [("ps", b)], [("stg", sl, half)])
            dma("sp", out_d[vc, tb * 128:(tb + 1) * 128, :], stg[sl][:, :],
                [("stg", sl, 0), ("stg", sl, 1)], [("out", vc, tb)], f"o{sl}")

    def rms_stats(src, skey, n, par):
        c0 = n * NT
        rk = [(skey, kc, n) for kc in range(KC)]
        sch.add("act", lambda e: e.activation(out=sqb[par][:, 0:4, :], in_=src[:, 0:4, c0:c0 + NT], func=AF.Square),
                rk[0:4], [("sqb", par, 0)])
        sch.add("act", lambda e: e.activation(out=sqb[par][:, 4:8, :], in_=src[:, 4:8, c0:c0 + NT], func=AF.Square),
                rk[4:8], [("sqb", par, 1)])
        b = nb()
        for kc in range(KC):
            mm(ps[b][:, 0:NT], onesm[:, :], sqb[par][:, kc, :], kc == 0, kc == KC - 1,
               [("sqb", par, kc // 4), ("onesm",)], [("ps", b)])
        act(sd_t[par][:, :], ps[b][:, 0:NT], AF.Sqrt, [("ps", b), ("epsr",)], [("sd", par)], bias=epsr[:, 0:1])
        sch.add("dve", lambda e: e.reciprocal(out=rstd_t[par][:, :], in_=sd_t[par][:, :]),
                [("sd", par)], [("rstd", par)])

    def pre_norm(i, which, n, par, dst, dkey, off):
        rms_stats(xres, "x", n, par)
        c0 = n * NT
        for kc in range(KC):
            t = ntmp()
            tt("dve" if kc % 2 == 0 else PN_ENG, tmpf[t][:, 0:NT], xres[:, kc, c0:c0 + NT], rstd_t[par][:, :], ALU.mult,
               [("x", kc, n), ("rstd", par)], [("tmp", t)])
            act(dst[:, kc, off + c0:off + c0 + NT], tmpf[t][:, 0:NT], AF.Identity,
                [("tmp", t), ("lvec", i, which // 3 * 2)], [(dkey, kc, n)],
                scale=LV(i, which, kc), bias=LV(i, which + 1, kc))
        if n == 0:
            for kc in range(KC):
                tt(MASK_ENG, dst[:, kc, off:off + H], dst[:, kc, off:off + H], maskt[:, LM:LM + H], ALU.mult,
                   [(dkey, kc, 0), ("mask",)], [(dkey, kc, 0)])

    def post_norm(i, which, n, par):
        rms_stats(ybuf, "y", n, par)
        c0 = n * NT
        tl = {}

        def p1(kc):
            tl[kc] = ntmp()
            tt(PN_ENG, tmpf[tl[kc]][:, 0:NT], ybuf[:, kc, c0:c0 + NT], rstd_t[par][:, :], ALU.mult,
               [("y", kc, n), ("rstd", par)], [("tmp", tl[kc])])

        def p2(kc):
            t = tl[kc]
            stt(xres[:, kc, c0:c0 + NT], tmpf[t][:, 0:NT], LV(i, which, kc), xres[:, kc, c0:c0 + NT],
                ALU.mult, ALU.add, [("tmp", t), ("lvec", i, which // 3 * 2 + 1), ("x", kc, n)], [("x", kc, n)])
        p1(0)
        p1(1)
        for kc in range(KC):
            p2(kc)
            if kc + 2 < KC:
                p1(kc + 2)

    def run_pipelined(stage, cbs):
        post, pre = cbs
        stage(0)
        stage(1)
        post(0, 0)
        bg_step(bg_rate[2])
        stage(2)
        post(1, 1)
        bg_step(bg_rate[2])
        pre(0, 0)
        bg_step(bg_rate[2])
        post(2, 1)
        bg_step(bg_rate[2])
        pre(1, 0)
        bg_step(bg_rate[2])
        pre(2, 1)
        bg_step(bg_rate[2])

    def hkeys(key, kc, n, E):
        ks = [(key, kc, n)]
        if E > 0:
            ks.append((key, kc, n - 1) if n > 0 else ("hm",))
        return ks

    def ffn(i, vc, after_y):
        arena_barrier()
        E = 2
        fdw = VOFF[f"fdw{i}"]
        fdb = VOFF[f"fdb{i}"]
        pend = []
        for hh in range(2):
            for jj in range(NHP):
                j = hh * NHP + jj
                wt, wk = use_w(wup[i, j], 2048)
                for n in range(NSUB):
                    c0 = n * NT
                    bg_, bv_ = nb(), nb()
                    for half, b in ((0, bg_), (1, bv_)):
                        for kc in range(KC):
                            mm(ps[b][:, 0:NT + E],
                               wt[:, kc * 256 + half * 128: kc * 256 + half * 128 + 128],
                               hb[:, kc, LM + c0 - E: LM + c0 + NT], kc == 0, kc == KC - 1,
                               [wk] + hkeys("h", kc, n, E), [("ps", b)])
                    res = []
                    hb_ = ((0, bg_), (1, bv_))
                    for half, b in hb_:
                        ch = half * NPAIR + j
                        t = ntmp()
                        res.append(t)
                        act(tmpf[t][:, 0:NT], ps[b][:, 2:NT + 2], AF.Identity, [("ps", b), ("vecs",)], [("tmp", t)],
                            scale=vecs[:, fdw + 2 * 44 + ch: fdw + 2 * 44 + ch + 1], bias=vecs[:, fdb + ch: fdb + ch + 1])
                    for tap in (1, 0):
                        for half, b in hb_:
                            ch = half * NPAIR + j
                            t = res[half]
                            stt(tmpf[t][:, 0:NT], ps[b][:, tap:NT + tap], vecs[:, fdw + tap * 44 + ch: fdw + tap * 44 + ch + 1],
                                tmpf[t][:, 0:NT], ALU.mult, ALU.add, [("ps", b), ("vecs",), ("tmp", t)], [("tmp", t)])
                    tg, tv = res
                    t2 = tg
                    act(tmpf[t2][:, 0:NT], tmpf[tg][:, 0:NT], AF.Silu, [("tmp", tg)], [("tmp", t2)])
                    if pend:
                        pend.pop(0)()
                    pend.append(lambda jj=jj, c0=c0, t2=t2, tv=tv, n=n: tt(
                        GM_ENG, gbuf[:, jj, c0:c0 + NT], tmpf[t2][:, 0:NT], tmpf[tv][:, 0:NT], ALU.mult,
                        [("tmp", t2), ("tmp", tv)], [("g", jj, n)]))
                    bg_step(bg_rate[0])
            while pend:
                pend.pop(0)()
            if hh == 1:
                wg = use_w_group([wdn[i, m][:, NHP * 128:2 * NHP * 128] for m in range(KC)], NHP * 128)

                def down(n):
                    c0 = n * NT
                    for m in range(KC):
                        wt, wo, wk = wg[m]
                        b = nb()
                        for jj in range(NHP):
                            mm(ps[b][:, 0:NT], wt[:, wo + jj * 128: wo + (jj + 1) * 128], gbuf[:, jj, c0:c0 + NT],
                               jj == 0, jj == NHP - 1, [wk, ("g", jj, n)], [("ps", b)])
                        tt("dve", ybuf[:, m, c0:c0 + NT], ybuf[:, m, c0:c0 + NT], ps[b][:, 0:NT], ALU.add,
                           [("ps", b), ("y", m, n)], [("y", m, n)])
                run_pipelined(down, after_y)
                continue
            for m in range(KC):
                wt, wk = use_w(wdn[i, m][:, hh * NHP * 128:(hh + 1) * NHP * 128], NHP * 128)
                for n in range(NSUB):
                    c0 = n * NT
                    b = nb()
                    for jj in range(NHP):
                        mm(ps[b][:, 0:NT], wt[:, jj * 128:(jj + 1) * 128], gbuf[:, jj, c0:c0 + NT],
                           jj == 0, jj == NHP - 1, [wk, ("g", jj, n)], [("ps", b)])
                    if hh == 0:
                        cp("act", ybuf[:, m, c0:c0 + NT], ps[b][:, 0:NT], [("ps", b)], [("y", m, n)])
                    else:
                        tt("dve", ybuf[:, m, c0:c0 + NT], ybuf[:, m, c0:c0 + NT], ps[b][:, 0:NT], ALU.add,
                           [("ps", b), ("y", m, n)], [("y", m, n)])
        dump(f"yffn{i}", ybuf[:], YK)
        dump(f"xffn{i}", xres[:], XK)

    def mixer_b(i, vc, after_y):
        dump(f"hmix{i}", hf[:], YK)

        def pool(n):
            c0 = n * NT
            E = 15 if n > 0 else 0
            W = NT + E
            for kc in range(KC):
                g = kc // 2
                ta, tb_ = ntmp(), ntmp()
                s0 = hf[:, kc, c0 - E: c0 + NT]
                rk = hkeys("y", kc, n, E)
                A_ = tmpf[ta]
                B_ = tmpf[tb_]
                tt("dve", A_[:, 1:W], s0[:, 1:W], s0[:, 0:W - 1], ALU.add, rk, [("tmp", ta)])
                cur, curk = A_, ta
                if g >= 1:
                    tt("dve", B_[:, 3:W], A_[:, 3:W], A_[:, 1:W - 2], ALU.add, [("tmp", ta)], [("tmp", tb_)])
                    cur, curk = B_, tb_
                if g >= 2:
                    tt("dve", A_[:, 7:W], B_[:, 7:W], B_[:, 3:W - 4], ALU.add, [("tmp", tb_), ("tmp", ta)], [("tmp", ta)])
                    cur, curk = A_, ta
                if g >= 3:
                    tt("dve", B_[:, 15:W], A_[:, 15:W], A_[:, 7:W - 8], ALU.add, [("tmp", ta), ("tmp", tb_)], [("tmp", tb_)])
                    cur, curk = B_, tb_
                wv = float(2 ** (g + 1))
                lo = 15 - E
                stt(hb[:, kc, LM + c0 + lo: LM + c0 + NT], cur[:, E + lo:W], 1.0 / wv, hf[:, kc, c0 + lo:c0 + NT],
                    ALU.mult, ALU.subtract, [("tmp", curk), ("y", kc, n)], [("h", kc, n)])
                if n == 0:
                    memset("dve", hb[:, kc, LM:LM + lo], 0.0, [("h", kc, 0)])
                    t3 = ntmp()
                    tt("dve", tmpf[t3][:, 0:16], cur[:, H:H + 16], invct[:, g * 16:(g + 1) * 16], ALU.mult,
                       [("tmp", curk), ("invc",)], [("tmp", t3)])
                    tt("dve", hb[:, kc, LM + H:LM + H + 16], tmpf[t3][:, 0:16], hf[:, kc, H:H + 16],
                       ALU.subtract, [("tmp", t3), ("y", kc, 0), ("h", kc, 0)], [("h", kc, 0)])

        def gmm(n):
            c0 = n * NT
            for m in range(KC):
                g = m // 2
                b = nb()
                for kk in range(2):
                    kc = 2 * g + kk
                    o = g * 512 + kk * 256 + (m % 2) * 128
                    mm(ps[b][:, 0:NT], bgws[:, o:o + 128], hb[:, kc, LM + c0:LM + c0 + NT], kk == 0, kk == 1,
                       [("bgws",), ("h", kc, n)], [("ps", b)])
                act(ybuf[:, m, c0:c0 + NT], ps[b][:, 0:NT], AF.Identity, [("ps", b), ("vecs",), ("bsv",)], [("y", m, n)],
                    scale=V("bsc", 1, m), bias=bsv[:, m:m + 1])

        pool(0)

        def stage(n):
            if n + 1 < NSUB:
                pool(n + 1)
            gmm(n)
        run_pipelined(stage, after_y)

    def mixer_c(i, vc, after_y):
        arena_barrier()
        E = 2
        ccw = VOFF["ccw"]
        for m in range(KC):
            wt, wk = use_w(cwin[m], 3072)
            for n in range(NSUB):
                c0 = n * NT
                bs = [nb(), nb(), nb()]
                for t3 in range(3):
                    for kc in range(KC):
                        mm(ps[bs[t3]][:, 0:NT + E], wt[:, kc * 384 + t3 * 128: kc * 384 + t3 * 128 + 128],
                           hb[:, kc, LM + c0 - E: LM + c0 + NT], kc == 0, kc == KC - 1,
                           [wk] + hkeys("h", kc, n, E), [("ps", bs[t3])])
                tv, tz, tc = ntmp(), ntmp(), ntmp()
                cp("act", tmpf[tv][:, 0:NT + E], ps[bs[2]][:, 0:NT + E], [("ps", bs[2])], [("tmp", tv)])
                tt("dve", tmpf[tz][:, 0:NT + E], ps[bs[1]][:, 0:NT + E], tmpf[tv][:, 0:NT + E], ALU.mult,
                   [("ps", bs[1]), ("tmp", tv)], [("tmp", tz)])
                w = lambda k, m=m: vecs[:, ccw + k * 8 + m: ccw + k * 8 + m + 1]
                act(tmpf[tc][:, 0:NT], tmpf[tz][:, 2:NT + 2], AF.Identity, [("tmp", tz), ("vecs",)], [("tmp", tc)],
                    scale=w(2))
                stt(tmpf[tc][:, 0:NT], tmpf[tz][:, 1:NT + 1], w(1), tmpf[tc][:, 0:NT], ALU.mult, ALU.add,
                    [("tmp", tz), ("vecs",), ("tmp", tc)], [("tmp", tc)])
                stt(tmpf[tc][:, 0:NT], tmpf[tz][:, 0:NT], w(0), tmpf[tc][:, 0:NT], ALU.mult, ALU.add,
                    [("tmp", tz), ("vecs",), ("tmp", tc)], [("tmp", tc)])
                tt("dve", rbuf[:, m, c0:c0 + NT], ps[bs[0]][:, 2:NT + 2], tmpf[tc][:, 0:NT], ALU.mult,
                   [("ps", bs[0]), ("tmp", tc)], [("r", m, n)])
        wg = use_w_group([cwout[m] for m in range(KC)], 1024)
        def stage(n):
            c0 = n * NT
            for m in range(KC):
                wt, wo, wk = wg[m]
                b = nb()
                for kc in range(KC):
                    mm(ps[b][:, 0:NT], wt[:, wo + kc * 128: wo + (kc + 1) * 128], rbuf[:, kc, c0:c0 + NT],
                       kc == 0, kc == KC - 1, [wk, ("r", kc, n)], [("ps", b)])
                cp("act", ybuf[:, m, c0:c0 + NT], ps[b][:, 0:NT], [("ps", b)], [("y", m, n)])
        run_pipelined(stage, after_y)

    def mixer_a(i, vc, after_y):
        slot = i // 3
        arena_barrier()
        E = 30
        ab1 = VOFF[f"ab1{slot}"]
        adw = VOFF[f"adw{slot}"]
        units = [(m, n) for m in range(KC) for n in range(NSUB)]
        state = {}

        def emit_pw1(idx):
            m, n = units[idx]
            if n == 0:
                state["w"] = use_w(aw1[slot, m], 2048)
                dpar = m % 2
                for k in range(31):
                    ts("dve", diag[dpar][:, k, :], identb[:, :], vecs[:, adw + k * 8 + m: adw + k * 8 + m + 1], None,
                       ALU.mult, ALU.bypass, [("identb",), ("vecs",)], [("diag", dpar)])
            wt, wk = state["w"]
            c0 = n * NT
            ba, bg2 = nb(), nb()
            for half, b_ in ((0, ba), (1, bg2)):
                for kc in range(KC):
                    mm(ps[b_][:, 0:NT + E],
                       wt[:, kc * 256 + half * 128: kc * 256 + half * 128 + 128],
                       hb[:, kc, LM + c0 - E: LM + c0 + NT], kc == 0, kc == KC - 1,
                       [wk] + hkeys("h", kc, n, E), [("ps", b_)])
            gp = idx % 2
            t = ntmp()
            act(tmpf[t][:, 0:NT + E], ps[bg2][:, 0:NT + E], AF.Sigmoid, [("ps", bg2), ("vecs",)], [("tmp", t)],
                bias=vecs[:, ab1 + 8 + m: ab1 + 8 + m + 1])
            stt(glu[gp][:, 0:NT + E], ps[ba][:, 0:NT + E], vecs[:, ab1 + m: ab1 + m + 1], tmpf[t][:, 0:NT + E],
                ALU.add, ALU.mult, [("ps", ba), ("vecs",), ("tmp", t)], [("glu", gp)])
            if n == 0:
                tt(MASK_ENG, glu[gp][:, 0:E + H], glu[gp][:, 0:E + H], maskt[:, LM - E:LM + H], ALU.mult,
                   [("glu", gp), ("mask",)], [("glu", gp)])
            bg_step(bg_rate[1])

        def emit_conv(idx):
            m, n = units[idx]
            c0 = n * NT
            gp = idx % 2
            dpar = m % 2
            bc = nb()
            for k in range(31):
                mm(ps[bc][:, 0:NT], diag[dpar][:, k, :], glu[gp][:, k:k + NT], k == 0, k == 30,
                   [("diag", dpar), ("glu", gp)], [("ps", bc)])
            act(u2[:, m, c0:c0 + NT], ps[bc][:, 0:NT], AF.Identity, [("ps", bc), ("vecs",)], [("y", m, n)],
                bias=V(f"adb{slot}", 1, m))

        for idx in range(len(units) + 1):
            if idx < len(units):
                emit_pw1(idx)
            if idx >= 1:
                emit_conv(idx - 1)
        dump(f"aconv{i}", u2[:], YK)
        wg = use_w_group([aw2[slot, m] for m in range(KC)], 1024)
        def ln_stats(n):
            c0 = n * NT
            par = n % 2
            rk = [("y", kc, n) for kc in range(KC)]
            sch.add("act", lambda e, c0=c0, par=par: e.activation(out=sqb[par][:, :, :], in_=u2[:, :, c0:c0 + NT],
                                                                  func=AF.Square), rk, [("sqb", par, 0), ("sqb", par, 1)])
            cp("dve", sqb[1 - par][:, :, :], u2[:, :, c0:c0 + NT], rk, [("sqb", 1 - par, 0), ("sqb", 1 - par, 1)])
            bm, be = nb(), nb()
            for kc in range(KC):
                mm(ps[bm][:, 0:NT], onesm[:, :], sqb[1 - par][:, kc, :], kc == 0, kc == KC - 1,
                   [("sqb", 1 - par, 0), ("sqb", 1 - par, 1), ("onesm",)], [("ps", bm)])
            for kc in range(KC):
                mm(ps[be][:, 0:NT], onesm[:, :], sqb[par][:, kc, :], kc == 0, kc == KC - 1,
                   [("sqb", par, 0), ("sqb", par, 1), ("onesm",)], [("ps", be)])
            t = ntmp()
            act(tmpf[t][:, 0:NT], ps[bm][:, 0:NT], AF.Square, [("ps", bm)], [("tmp", t)])
            tt("dve", tmpf[t][:, 0:NT], ps[be][:, 0:NT], tmpf[t][:, 0:NT], ALU.subtract,
               [("ps", be), ("tmp", t)], [("tmp", t)])
            ts("dve", tmpf[t][:, 0:NT], tmpf[t][:, 0:NT], 0.0, None, ALU.max, ALU.bypass, [("tmp", t)], [("tmp", t)])
            act(sd_t[par][:, :], tmpf[t][:, 0:NT], AF.Sqrt, [("tmp", t), ("epsl",)], [("sd", par)], bias=epsl[:, 0:1])
            sch.add("dve", lambda e, par=par: e.reciprocal(out=rstd_t[par][:, :], in_=sd_t[par][:, :]),
                    [("sd", par)], [("rstd", par)])
            stt(nmr_t[par][:, :], ps[bm][:, 0:NT], -1.0, rstd_t[par][:, :], ALU.mult, ALU.mult,
                [("ps", bm), ("rstd", par)], [("nmr", par)])

        def ln_apply(n):
            c0 = n * NT
            par = n % 2
            tl = {}

            def q1(kc):
                tl[kc] = ntmp()
                tt("dve", tmpf[tl[kc]][:, 0:NT], u2[:, kc, c0:c0 + NT], rstd_t[par][:, :], ALU.mult,
                   [("y", kc, n), ("rstd", par)], [("tmp", tl[kc])])

            def q2(kc):
                t1 = tl[kc]
                tt(PN_ENG, tmpf[t1][:, 0:NT], tmpf[t1][:, 0:NT], nmr_t[par][:, :], ALU.add,
                   [("tmp", t1), ("nmr", par)], [("tmp", t1)])
                act(hb[:, kc, LM + c0:LM + c0 + NT], tmpf[t1][:, 0:NT], AF.Silu, [("tmp", t1), ("vecs",)],
                    [("h", kc, n)], scale=V(f"alg{slot}", 1, kc), bias=V(f"alb{slot}", 1, kc))
            q1(0)
            q1(1)
            for kc in range(KC):
                q2(kc)
                if kc + 2 < KC:
                    q1(kc + 2)

        def pw2(n):
            c0 = n * NT
            for m in range(KC):
                wt, wo, wk = wg[m]
                b = nb()
                for kc in range(KC):
                    mm(ps[b][:, 0:NT], wt[:, wo + kc * 128: wo + (kc + 1) * 128], hb[:, kc, LM + c0:LM + c0 + NT],
                       kc == 0, kc == KC - 1, [wk, ("h", kc, n)], [("ps", b)])
                act(ybuf[:, m, c0:c0 + NT], ps[b][:, 0:NT], AF.Identity, [("ps", b), ("vecs",)], [("y", m, n)],
                    bias=V(f"ab2{slot}", 1, m))

        ln_stats(0)
        ln_apply(0)

        def stage(n):
            if n + 1 < NSUB:
                ln_stats(n + 1)
            pw2(n)
            if n + 1 < NSUB:
                ln_apply(n + 1)
        run_pipelined(stage, after_y)

    for t_ in range(NTMP):
        memset("dve", tmpf[t_][:], 0.0, [("tmp", t_)])
    queue_mod(0)
    bg_until((0, 0))

    def pre_norm_mixer(i, n, par):
        if i % 3 == 1:
            pre_norm(i, 0, n, par, hf, "y", 0)
        else:
            pre_norm(i, 0, n, par, hb, "h", LM)

    for vc in range(cfg.nvc):
        nbanks[0] = 7 if vc == 0 else 8
        dma("sp", maskt[:], mask_d[vc], (), [("mask",)], "mk")
        dma("sp", invct[:], invc_d[vc], (), [("invc",)], "iv")
        load_x(vc)
        dump("x0", xres[:], XK)
        for n in range(NSUB):
            pre_norm_mixer(0, n, n % 2)
        for i in range(cfg.nlayers):
            if vc == 0 and i + 1 < cfg.nlayers:
                queue_mod(i + 1)
                if not cfg.mod_bg:
                    bg_step(10 ** 6)
            bg_rate[0] = 0
            bg_rate[1] = 3 if (vc == 0 and i == 0) else 0
            bg_rate[2] = 7 if vc == 0 else 0
            if i + 1 >= cfg.nlayers and not bg_tasks:
                nbanks[0] = 8

            after_mix = (lambda n, par, i=i: post_norm(i, 2, n, par),
                         lambda n, par, i=i: pre_norm(i, 3, n, par, hb, "h", LM))
            after_ffn = (lambda n, par, i=i: post_norm(i, 5, n, par),
                         lambda n, par, i=i: (pre_norm_mixer(i + 1, n, par) if i + 1 < cfg.nlayers else None))
            kind = i % 3
            if kind == 0:
                mixer_a(i, vc, after_mix)
            elif kind == 1:
                mixer_b(i, vc, after_mix)
            else:
                mixer_c(i, vc, after_mix)
            dump(f"xmix{i}", xres[:], XK)
            ffn(i, vc, after_ffn)
            bg_step(10 ** 6)
        store_out(vc)
    final_reads = [("out", vc, tb) for vc in range(cfg.nvc) for tb in range(SV // 128)] + \
                  [("dump", nm) for nm in cfg.dumps]
    sch.add("sp", None, final_reads, ())

    ins = {}

    def mk_dma(out, in_, writes, key):
        op = Op("pool", lambda e: e.dma_start(out=out, in_=in_), (), tuple(writes), key)
        return op
    def last_uses(pieces, keyname, nslots):
        firsts = [p[0] for p in pieces]
        owner = {}
        nxt = 0
        lu = [p[0] for p in pieces]
        for idx, op in enumerate(sch.ops):
            while nxt < len(pieces) and firsts[nxt] <= idx:
                owner[pieces[nxt][1]] = nxt
                nxt += 1
            for kk in op.reads:
                if kk[0] == keyname and kk[1] in owner:
                    lu[owner[kk[1]]] = idx
        return lu

    wlu = last_uses(wp, "wslot", NWS)
    mlu = last_uses(mp, "mslot", MSLOT)
    for k, (pos, sl, subs) in enumerate(wp):
        at = wp[k - WDEPTH][0] if k >= WDEPTH else 0
        if k >= NWS:
            at = max(at, wlu[k - NWS] + 1)
        for (off_, src2d, ncols) in subs:
            nsp = (ncols + 2047) // 2048
            step = (ncols + nsp - 1) // nsp
            a_ = 0
            while a_ < ncols:
                b_ = min(ncols, a_ + step)
                ins.setdefault(at, []).append(mk_dma(wsl[sl][:, off_ + a_:off_ + b_], src2d[:, a_:b_],
                                                     [("wslot", sl)], f"w{sl}"))
                a_ = b_
    for q, (pos, sl, src2d) in enumerate(mp):
        at = mp[q - MDEPTH][0] if q >= MDEPTH else 0
        if q >= MSLOT:
            at = max(at, mlu[q - MSLOT] + 1)
        ins.setdefault(at, []).append(mk_dma(mslot[sl][:, :], src2d, [("mslot", sl)], f"m{sl}"))
    new_ops = []
    for idx, op in enumerate(sch.ops):
        if idx in ins:
            new_ops.extend(ins[idx])
        new_ops.append(op)
    sch.ops = new_ops
    for idx, op in enumerate(sch.ops):
        op.idx = idx

    sch.analyze()
    streams = sorted({sch.stream_of(op) for op in sch.ops if op.signal}, key=str)
    sems = {}
    sctx = []
    for st in streams:
        g = nc.semaphore("s_" + (st if isinstance(st, str) else "dma_" + st[1]))
        sems[st] = g.__enter__()
        sctx.append(g)
    blk = nc.Block()
    block = blk.__enter__()
    sch.emit_all(nc, block, sems)
    blk.__exit__(None, None, None)
    for g in reversed(sctx):
        g.__exit__(None, None, None)
    for g in reversed(ctx):
        g.__exit__(None, None, None)
    return nc, sch


def prep_shared(inp):
    f = np.float32
    sh = {}
    up = np.asarray(inp["f_up_w"], f).reshape(DEPTH, KC, 128, 2, NPAIR, 128)
    sh["wup"] = np.ascontiguousarray(up.transpose(0, 4, 2, 1, 3, 5)).reshape(DEPTH, NPAIR, 128, 2048)
    dn = np.asarray(inp["f_down_w"], f).reshape(DEPTH, NPAIR, 128, KC, 128)
    sh["wdn"] = np.ascontiguousarray(dn.transpose(0, 3, 2, 1, 4)).reshape(DEPTH, KC, 128, NPAIR * 128)
    w1 = np.asarray(inp["a_pw1_w"], f).reshape(2, KC, 128, 2, KC, 128)
    sh["aw1"] = np.ascontiguousarray(w1.transpose(0, 4, 2, 1, 3, 5)).reshape(2, KC, 128, 2048)
    w2 = np.asarray(inp["a_pw2_w"], f).reshape(2, KC, 128, KC, 128)
    sh["aw2"] = np.ascontiguousarray(w2.transpose(0, 3, 2, 1, 4)).reshape(2, KC, 128, 1024)
    ci = np.asarray(inp["c_in_w"], f)[0].reshape(KC, 128, 3, KC, 128)
    sh["cwin"] = np.ascontiguousarray(ci.transpose(3, 1, 0, 2, 4)).reshape(KC, 128, 3072)
    co = np.asarray(inp["c_out_w"], f)[0].reshape(KC, 128, KC, 128)
    sh["cwout"] = np.ascontiguousarray(co.transpose(2, 1, 0, 3)).reshape(KC, 128, 1024)
    gw = np.asarray(inp["b_group_w"], f)[0].reshape(4, 2, 128, 256)
    sh["bgw"] = np.ascontiguousarray(gw.transpose(2, 0, 1, 3)).reshape(128, 2048)
    mw = np.asarray(inp["mod_w"], f).reshape(DEPTH, KC, 128, NMODP, MODP)
    sh["modw"] = np.ascontiguousarray(mw.transpose(0, 3, 2, 1, 4)).reshape(DEPTH, NMODP, 128, KC * MODP)
    vec = np.zeros((128, NV), f)

    def put(name, arr):
        a = np.asarray(arr, f)
        vec[:, VOFF[name]:VOFF[name] + a.shape[1]] = a
    for i in range(DEPTH):
        put(f"npm{i}", _pk(inp["norm_pre_mix"][i]))
        put(f"nqm{i}", _pk(inp["norm_post_mix"][i]))
        put(f"npf{i}", _pk(inp["norm_pre_ffn"][i]))
        put(f"nqf{i}", _pk(inp["norm_post_ffn"][i]))
        put(f"modb{i}", _pk(inp["mod_b"][i]))
        fd = np.asarray(inp["f_dw_w"][i], f)
        put(f"fdw{i}", np.concatenate([_pk(fd[k]) for k in range(3)], axis=1))
        put(f"fdb{i}", _pk(inp["f_dw_b"][i]))
    for s in range(2):
        put(f"ab1{s}", _pk(inp["a_pw1_b"][s]))
        ad = np.asarray(inp["a_dw_w"][s], f)
        put(f"adw{s}", np.concatenate([_pk(ad[k]) for k in range(31)], axis=1))
        put(f"adb{s}", _pk(inp["a_dw_b"][s]))
        put(f"alg{s}", _pk(inp["a_ln_g"][s]))
        put(f"alb{s}", _pk(inp["a_ln_b"][s]))
        put(f"ab2{s}", _pk(inp["a_pw2_b"][s]))
    put("bgb", _pk(inp["b_group_b"][0]))
    put("bsc", _pk(inp["b_scale"][0]))
    cc = np.asarray(inp["c_conv_w"][0], f)
    put("ccw", np.concatenate([_pk(cc[k]) for k in range(3)], axis=1))
    sh["vecs"] = vec
    sh["ident"] = np.eye(128, dtype=f)
    return sh


def prep_core(inp, core):
    f = np.float32
    b = core // 4
    q = core % 4
    x = np.asarray(inp["x"], f)
    xin = np.zeros((NVC, S, D), f)
    mask = np.ones((NVC, 128, LM + H), f)
    invc = np.zeros((NVC, 128, 64), f)
    for vc in range(NVC):
        p0 = q * 2048 + vc * SV
        lo = p0 - H
        if lo >= 0:
            xin[vc] = x[b, lo:p0 + SV]
        else:
            xin[vc, -lo:] = x[b, 0:p0 + SV]
            mask[vc] = 0.0
        for g, w in enumerate((2, 4, 8, 16)):
            pos = p0 + np.arange(16) + 1
            invc[vc, :, g * 16:(g + 1) * 16] = (1.0 / np.minimum(pos, w)).astype(f)[None, :]
    cvec = _pk(inp["c"][b])
    return {"xin": xin, "mask": mask, "invc": invc, "cvec": cvec}


_PROG = {}


def kernel(**inputs):
    if "full" not in _PROG:
        _PROG["full"] = build_program(Cfg())[0]
    nc = _PROG["full"]
    sh = prep_shared(inputs)
    in_maps = []
    for core in range(NCORES):
        m = dict(sh)
        m.update(prep_core(inputs, core))
        in_maps.append(m)
    res = run_bass_kernel_spmd(nc, in_maps, core_ids=list(range(NCORES)))
    out = np.zeros((BATCH, SEQ, D), np.float32)
    for core in range(NCORES):
        b = core // 4
        q = core % 4
        o = res.results[core]["out"]
        for vc in range(NVC):
            p0 = q * 2048 + vc * SV
            out[b, p0:p0 + SV] = o[vc]
    return out
```
